# Optimizing a Trainium2 kernel written in Bass

```python
import jax, jax.numpy as jnp
from jax import lax
import numpy as np

D_MODEL = 1024
BATCH = 32
SEQ = 2048
DEPTH = 1

D_PLE = 256
RET_HEAD_DIM = 128
RET_WIDTH = D_MODEL // 2
RET_HEADS = RET_WIDTH // RET_HEAD_DIM
RET_CHUNK = 128
RWKV_HEAD_DIM = 64
RWKV_WIDTH = D_MODEL - RET_WIDTH
RWKV_HEADS = RWKV_WIDTH // RWKV_HEAD_DIM
DECAY_LORA = 64
AAA_LORA = 64
GATE_LORA = 128
D_FF = ((8 * D_MODEL + 3 * 256 - 1) // (3 * 256)) * 256
RET_COLS = 4 * RET_WIDTH
RWKV_COLS = 3 * RWKV_WIDTH + DECAY_LORA + AAA_LORA + GATE_LORA
IN_COLS = RET_COLS + RWKV_COLS
NORM_EPS = 1e-6
RET_GN_EPS = 1e-5
RWKV_GN_EPS = 64e-5

kernel_name = "hybrid_retention_rwkv7_parallel_heads"


def rms_norm(x, w):
    x32 = x.astype(jnp.float32)
    y = x32 * lax.rsqrt(jnp.mean(x32 * x32, axis=-1, keepdims=True) + NORM_EPS)
    return (y * w.astype(jnp.float32)).astype(x.dtype)


def head_norm(y, eps):
    mu = jnp.mean(y, axis=-1, keepdims=True)
    var = jnp.mean(jnp.square(y - mu), axis=-1, keepdims=True)
    return (y - mu) * lax.rsqrt(var + eps)


def rotate_every_two(t):
    t1 = t[..., 0::2]
    t2 = t[..., 1::2]
    return jnp.stack((-t2, t1), axis=-1).reshape(t.shape)


def retention(q, k, v):
    B, S, H, d = q.shape
    C = RET_CHUNK
    N = S // C
    pos = jnp.arange(S, dtype=jnp.float32)
    angle = 1.0 / (10000.0 ** jnp.linspace(0.0, 1.0, d // 2, dtype=jnp.float32))
    angle = jnp.repeat(angle, 2)
    phase = pos[:, None] * angle[None, :]
    sin = jnp.sin(phase)[:, None, :]
    cos = jnp.cos(phase)[:, None, :]
    q = q * cos + rotate_every_two(q) * sin
    k = (k * cos + rotate_every_two(k) * sin) * (d ** -0.5)
    log_g = jnp.log(1.0 - 2.0 ** (-5.0 - jnp.arange(H, dtype=jnp.float32)))

    def chunks(t):
        return t.reshape(B, N, C, H, t.shape[-1]).transpose(0, 3, 1, 2, 4)

    qc, kc, vc = chunks(q), chunks(k), chunks(v)
    idx = jnp.arange(C, dtype=jnp.float32)
    rel = idx[:, None] - idx[None, :]
    inner_decay = jnp.where(rel[None] >= 0,
                            jnp.exp(log_g[:, None, None] * jnp.maximum(rel, 0.0)[None]),
                            0.0)
    scores = jnp.einsum('bhncd,bhnmd->bhncm', qc, kc) * inner_decay[None, :, None]
    inner = jnp.einsum('bhncm,bhnme->bhnce', scores, vc)
    k_to_end = jnp.exp(log_g[:, None] * (C - 1 - idx)[None])
    chunk_state = jnp.einsum('bhnmd,bhnme->bhnde',
                             kc * k_to_end[None, :, None, :, None], vc)
    chunk_decay = jnp.exp(log_g * C)[None, :, None, None]

    def step(R, U):
        return R * chunk_decay + U, R

    _, R_prev = lax.scan(step, jnp.zeros((B, H, d, vc.shape[-1]), jnp.float32),
                         chunk_state.transpose(2, 0, 1, 3, 4))
    R_prev = R_prev.transpose(1, 2, 0, 3, 4)
    q_from_start = jnp.exp(log_g[:, None] * (idx + 1.0)[None])
    cross = jnp.einsum('bhncd,bhnde->bhnce', qc, R_prev) * q_from_start[None, :, None, :, None]
    return (inner + cross).transpose(0, 2, 3, 1, 4).reshape(B, S, H, vc.shape[-1])


def rwkv7_mix(h, mu, w0, w2, a0, a2, g2, k_k, k_a, r_k, ln_w, ln_b):
    B, S, _ = h.shape
    H, N, W = RWKV_HEADS, RWKV_HEAD_DIM, RWKV_WIDTH
    prev = jnp.pad(h, ((0, 0), (1, 0), (0, 0)))[:, :S]
    h = h + (prev - h) * mu
    r = h[..., :W]
    k = h[..., W:2 * W]
    v = h[..., 2 * W:3 * W]
    o = 3 * W
    wd = h[..., o:o + DECAY_LORA]
    o += DECAY_LORA
    ad = h[..., o:o + AAA_LORA]
    o += AAA_LORA
    gd = h[..., o:o + GATE_LORA]
    w_log = -jax.nn.softplus(-(w0 + jnp.tanh(wd) @ w2)) - 0.5
    decay = jnp.exp(-jnp.exp(w_log))
    a = jax.nn.sigmoid(a0 + ad @ a2)
    g = jax.nn.sigmoid(gd) @ g2

    def heads(t):
        return t.reshape(B, S, H, N)

    r, k, v, decay, a = heads(r), heads(k), heads(v), heads(decay), heads(a)
    kk = k * k_k.reshape(H, N)
    kk = kk * lax.rsqrt(jnp.maximum(jnp.sum(kk * kk, axis=-1, keepdims=True), 1e-24))
    k = k * (1.0 + (a - 1.0) * k_a.reshape(H, N))

    def step(state, inp):
        r_t, w_t, k_t, v_t, kk_t, b_t = inp
        sa = jnp.einsum('bhvk,bhk->bhv', state, -kk_t)
        state = (state * w_t[:, :, None, :] + sa[..., None] * b_t[:, :, None, :]
                 + v_t[..., None] * k_t[:, :, None, :])
        return state, jnp.einsum('bhvk,bhk->bhv', state, r_t)

    def tm(t):
        return jnp.swapaxes(t, 0, 1)

    _, y = lax.scan(step, jnp.zeros((B, H, N, N), jnp.float32),
                    (tm(r), tm(decay), tm(k), tm(v), tm(kk), tm(kk * a)))
    y = tm(y)
    y = head_norm(y, RWKV_GN_EPS).reshape(B, S, W) * ln_w + ln_b
    bonus = jnp.sum(r * k * r_k, axis=-1, keepdims=True) * v
    return (y + bonus.reshape(B, S, W)) * g


def setup_inputs(seed: int = 0) -> dict:
    key = jax.random.key(seed)
    ks = jax.random.split(key, 32)
    f32 = jnp.float32

    def nrm(k, shape, scale):
        return jax.random.normal(k, shape, f32) * scale

    def gain(k, shape):
        return 1.0 + 0.02 * jax.random.normal(k, shape, f32)

    L = DEPTH
    return {
        "x": nrm(ks[0], (BATCH, SEQ, D_MODEL), 1.0),
        "p": nrm(ks[1], (DEPTH, BATCH, SEQ, D_PLE), 1.0),
        "norm_mix_w": gain(ks[2], (L, D_MODEL)),
        "w_in": nrm(ks[3], (L, D_MODEL, IN_COLS), D_MODEL ** -0.5),
        "ret_norm_w": gain(ks[4], (L, RET_WIDTH)),
        "rw_mu": jax.random.uniform(ks[5], (L, RWKV_COLS), f32, 0.0, 1.0),
        "rw_w0": jax.random.uniform(ks[6], (L, RWKV_WIDTH), f32, -6.0, 0.0),
        "rw_w2": nrm(ks[7], (L, DECAY_LORA, RWKV_WIDTH), 0.1 * DECAY_LORA ** -0.5),
        "rw_a0": nrm(ks[8], (L, RWKV_WIDTH), 0.1),
        "rw_a2": nrm(ks[9], (L, AAA_LORA, RWKV_WIDTH), 0.1 * AAA_LORA ** -0.5),
        "rw_g2": nrm(ks[10], (L, GATE_LORA, RWKV_WIDTH), GATE_LORA ** -0.5),
        "rw_k_k": 0.85 + 0.02 * jax.random.normal(ks[11], (L, RWKV_WIDTH), f32),
        "rw_k_a": gain(ks[12], (L, RWKV_WIDTH)),
        "rw_r_k": nrm(ks[13], (L, RWKV_HEADS, RWKV_HEAD_DIM), 0.1),
        "rw_ln_w": gain(ks[14], (L, RWKV_WIDTH)),
        "rw_ln_b": nrm(ks[15], (L, RWKV_WIDTH), 0.01),
        "w_o": nrm(ks[16], (L, D_MODEL, D_MODEL), D_MODEL ** -0.5),
        "norm_ffn_w": gain(ks[17], (L, D_MODEL)),
        "w_gate": nrm(ks[18], (L, D_MODEL, D_FF), D_MODEL ** -0.5),
        "w_up": nrm(ks[19], (L, D_MODEL, D_FF), D_MODEL ** -0.5),
        "w_down": nrm(ks[20], (L, D_FF, D_MODEL), D_FF ** -0.5),
        "norm_ple_w": gain(ks[21], (L, D_MODEL)),
        "w_ple_gate": nrm(ks[22], (L, D_MODEL, D_MODEL), D_MODEL ** -0.5),
        "w_ple_up": nrm(ks[23], (L, D_PLE, D_MODEL), D_PLE ** -0.5),
        "final_norm_w": gain(ks[24], (D_MODEL,)),
    }


def reference(x, p, norm_mix_w, w_in, ret_norm_w, rw_mu, rw_w0, rw_w2, rw_a0, rw_a2,
              rw_g2, rw_k_k, rw_k_a, rw_r_k, rw_ln_w, rw_ln_b, w_o, norm_ffn_w,
              w_gate, w_up, w_down, norm_ple_w, w_ple_gate, w_ple_up, final_norm_w):
    B, S, _ = x.shape
    for i in range(DEPTH):
        hn = rms_norm(x, norm_mix_w[i])
        proj = (hn @ w_in[i]).astype(jnp.float32)
        q = proj[..., 0:RET_WIDTH].reshape(B, S, RET_HEADS, RET_HEAD_DIM)
        k = proj[..., RET_WIDTH:2 * RET_WIDTH].reshape(B, S, RET_HEADS, RET_HEAD_DIM)
        v = proj[..., 2 * RET_WIDTH:3 * RET_WIDTH].reshape(B, S, RET_HEADS, RET_HEAD_DIM)
        ret_gate = proj[..., 3 * RET_WIDTH:RET_COLS]
        y_ret = head_norm(retention(q, k, v), RET_GN_EPS).reshape(B, S, RET_WIDTH)
        y_ret = y_ret * ret_norm_w[i] * jax.nn.silu(ret_gate)
        y_rw = rwkv7_mix(proj[..., RET_COLS:], rw_mu[i], rw_w0[i], rw_w2[i], rw_a0[i],
                         rw_a2[i], rw_g2[i], rw_k_k[i], rw_k_a[i], rw_r_k[i],
                         rw_ln_w[i], rw_ln_b[i])
        mixed = jnp.concatenate([y_ret, y_rw], axis=-1).astype(x.dtype)
        x = x + mixed @ w_o[i]
        hf = rms_norm(x, norm_ffn_w[i])
        x = x + (jax.nn.silu(hf @ w_gate[i]) * (hf @ w_up[i])) @ w_down[i]
        hp = rms_norm(x, norm_ple_w[i])
        x = x + (p[i] @ w_ple_up[i]) * jax.nn.sigmoid(hp @ w_ple_gate[i])
    return rms_norm(x, final_norm_w)
```

```python
import numpy as np
import concourse.bass as bass
import concourse.mybir as mybir
from concourse.bass_utils import run_bass_kernel_spmd

F32 = mybir.dt.float32
BF16 = mybir.dt.bfloat16
AF = mybir.ActivationFunctionType
ALU = mybir.AluOpType
AX = mybir.AxisListType

COMPUTE = ("pe", "act", "dve", "pool")


class Res:
    __slots__ = ("name", "last_w", "readers", "excl")

    def __init__(self, name, excl=False):
        self.name = name
        self.last_w = None
        self.readers = []
        self.excl = excl


class Op:
    __slots__ = ("eng", "fn", "deps", "signal", "semval", "dsem", "dtarget", "idx", "nm")

    def __init__(self, eng, fn, nm=""):
        self.eng = eng
        self.fn = fn
        self.deps = []
        self.signal = False
        self.semval = 0
        self.dsem = None
        self.dtarget = 0
        self.nm = nm


class Prog:
    def __init__(self, nc, n_dma_sems=8):
        self.nc = nc
        self.ops = []
        self.engs = {"pe": nc.tensor, "act": nc.scalar, "dve": nc.vector, "pool": nc.gpsimd,
                     "sp": nc.sync}
        self.n_dma_sems = n_dma_sems
        self._stack = []
        self.after = None
        self.last_on = {}
        self.dma_ops = []

    def keep(self, cm):
        h = cm.__enter__()
        self._stack.append(cm)
        return h

    def sb(self, name, shape, dt):
        return self.keep(self.nc.sbuf_tensor(name, list(shape), dt))

    def op(self, eng, fn, reads=(), writes=(), nm=""):
        o = Op(eng, fn, nm)
        o.idx = len(self.ops)
        deps = {}
        is_dma = eng not in COMPUTE

        def add(d, raw):
            if d is None or d is o:
                return
            if d.eng == eng and not is_dma:
                if not raw or eng == "pe":
                    return
            deps[id(d)] = d

        for r in reads:
            if r.excl:
                add(r.last_w, True)
                for x in r.readers:
                    add(x, True)
                r.last_w = o
                r.readers = []
            else:
                add(r.last_w, True)
                r.readers.append(o)
        for w in writes:
            add(w.last_w, w.excl)
            for x in w.readers:
                add(x, w.excl)
            w.last_w = o
            w.readers = []
        if self.after is not None:
            deps[id(self.after)] = self.after
        o.deps = list(deps.values())
        self.ops.append(o)
        self.last_on[eng] = o
        if is_dma:
            self.dma_ops.append(o)
        return o

    def mark(self):
        return len(self._stack)

    def barrier(self, fn):
        prev = [v for k, v in self.last_on.items() if k in COMPUTE] + self.dma_ops[-self.n_dma_sems:]
        o = self.op("sp", fn)
        have = {id(d) for d in o.deps}
        for d in prev:
            if id(d) not in have and d is not o:
                o.deps.append(d)
        self.after = o
        return o

    def release_to(self, mark):
        while len(self._stack) > mark:
            cm = self._stack.pop()
            cm.__exit__(None, None, None)

    def emit(self):
        nc = self.nc
        for o in self.ops:
            for d in o.deps:
                d.signal = True
        sems = {e: self.keep(nc.semaphore("s_" + e)) for e in COMPUTE}
        dsems = [self.keep(nc.semaphore("s_dma%d" % i)) for i in range(self.n_dma_sems)]
        cnt = {e: 0 for e in COMPUTE}
        ndma = 0
        dma_hist = []
        for o in self.ops:
            if o.eng in COMPUTE:
                if o.signal:
                    cnt[o.eng] += 1
                    o.semval = cnt[o.eng]
            else:
                k = ndma % self.n_dma_sems
                o.dsem = dsems[k]
                o.dtarget = 16 * (ndma // self.n_dma_sems + 1)
                dma_hist.append(o)
                ndma += 1
        seen = {}
        ndma = 0
        for o in self.ops:
            e = self.engs[o.eng]
            waits = []
            for d in o.deps:
                if d.eng in COMPUTE:
                    key = (o.eng, d.eng)
                    if seen.get(key, 0) >= d.semval:
                        continue
                    seen[key] = d.semval
                    waits.append((sems[d.eng], d.semval))
                else:
                    key = (o.eng, id(d.dsem))
                    if seen.get(key, 0) >= d.dtarget:
                        continue
                    seen[key] = d.dtarget
                    waits.append((d.dsem, d.dtarget))
            if o.eng not in COMPUTE:
                if ndma >= self.n_dma_sems:
                    p = dma_hist[ndma - self.n_dma_sems]
                    key = (o.eng, id(p.dsem))
                    if seen.get(key, 0) < p.dtarget:
                        seen[key] = p.dtarget
                        waits.append((p.dsem, p.dtarget))
                ndma += 1
            for s, v in waits:
                e.wait_ge(s, v)
            ins = o.fn(e)
            if o.eng in COMPUTE:
                if o.signal:
                    ins.then_inc(sems[o.eng], 1)
            else:
                ins.then_inc(o.dsem, 16)
        tail = self.engs["sp"]
        start = max(0, len(dma_hist) - self.n_dma_sems)
        for p in dma_hist[start:]:
            tail.wait_ge(p.dsem, p.dtarget)


D = 1024
SEQ = 2048
NCORE = 8
SEQ_PER_CORE = 4
DFF = 2816
NFF = DFF // 128
DPLE = 256
RW = 512
EPS = 1e-6
C_DEC = float(np.exp(-0.5))
DH = 128
LOG_G = [float(np.log(1.0 - 2.0 ** (-5.0 - h))) for h in range(4)]

PV = {}
_o = 0
for _n, _w in (("nw_mix", 8), ("ret_nw", 4), ("mu", 14), ("w0", 4), ("a0", 4), ("k_k", 4),
               ("k_a", 4), ("r_k", 4), ("nw_ffn", 8), ("nw_ple", 8)):
    PV[_n] = (_o, _w)
    _o += _w
NPV = _o

CS = {}
_o = 0
for _n, _w in (("ident", 128),
               ("maskT", 128), ("ghat", 4), ("epsr", 4), ("gC", 512), ("gCd", 512),
               ("mask4", 512), ("scanmask", 256), ("bones", 128), ("sel2", 2), ("bd4", 512)):
    CS[_n] = (_o, _w)
    _o += _w
NCS = _o
CA0, CA1 = CS["maskT"][0], CS["gCd"][0] + CS["gCd"][1]
CB0, CB1 = CS["mask4"][0], CS["bd4"][0] + CS["bd4"][1]


def host_consts():
    c = np.zeros((128, NCS), np.float32)

    def put(name, arr):
        o, w = CS[name]
        c[:, o:o + w] = arr.reshape(128, w)

    p = np.arange(128)
    put("ident", np.eye(128, dtype=np.float32))
    put("maskT", (p[:, None] <= p[None, :]).astype(np.float32) * (DH ** -0.5))
    lg = np.array(LOG_G, np.float64)
    put("ghat", np.exp(-lg[None, :] * (p[:, None] + 1.0)).astype(np.float32))
    put("epsr", (1e-5 * np.exp(-2.0 * lg[None, :] * (p[:, None] + 1.0))).astype(np.float32))
    gC = np.exp(lg * 128.0)
    put("gC", np.broadcast_to(np.repeat(gC, 128)[None, :], (128, 512)).astype(np.float32))
    put("gCd", np.broadcast_to(np.repeat(gC * DH ** -0.5, 128)[None, :], (128, 512)).astype(np.float32))
    same = (p[:, None] // 64) == (p[None, :] // 64)
    su = (same & (p[:, None] < p[None, :])).astype(np.float32)
    iu = (same & (p[:, None] <= p[None, :])).astype(np.float32)
    put("mask4", np.concatenate([su, iu, su, iu], axis=1))
    sm = np.ones((128, 256), np.float32)
    sm[:, 0::64] = 0.0
    put("scanmask", sm)
    put("bones", same.astype(np.float32))
    put("sel2", (p[:, None] // 64 == np.arange(2)[None, :]).astype(np.float32))
    put("bd4", np.tile(same.astype(np.float32), (1, 4)))
    pos = (np.arange(16)[None, :] * 128 + p[:, None]).astype(np.float32)
    angle = (1.0 / (10000.0 ** np.linspace(0.0, 1.0, DH // 2, dtype=np.float32))).astype(np.float32)
    angle = np.repeat(angle, 2)
    phase = (pos[:, :, None] * angle[None, None, :]).astype(np.float32)
    cos = np.cos(phase).astype(np.float32)
    sin = np.sin(phase).astype(np.float32)
    sgn = np.where(np.arange(DH) % 2 == 0, -1.0, 1.0).astype(np.float32)
    rot = np.stack([cos, sin * sgn[None, None, :]], axis=1).astype(np.float32)
    return c, np.ascontiguousarray(rot.reshape(128, 2 * 16 * 128))


class Tile:
    def __init__(self, P, name, shape, dt):
        self.h = P.sb(name, shape, dt)
        self.r = Res(name)

    def __getitem__(self, k):
        return self.h[k]


class Builder:
    def __init__(self, n_seq=SEQ_PER_CORE, n_tok=SEQ, phases=("1a", "1b", "2"), dbg=None):
        self.n_seq = n_seq
        self.n_tok = n_tok
        self.phases = phases
        self.dbg = dbg or ()
        self.dbg_outs = {}
        nc = bass.Bass("TRN2", target_bir_lowering=False)
        self.nc = nc
        self.P = Prog(nc, n_dma_sems=16)
        NT = n_seq * n_tok
        self.NT = NT

        def din(name, shape):
            return nc.dram_tensor(name, list(shape), F32, kind="ExternalInput").ap()

        self.x = din("x", [NT, D])
        self.p = din("p", [NT, DPLE])
        self.w_in = din("w_in", [D, 3840])
        self.w_o = din("w_o", [D, D])
        self.w_gate = din("w_gate", [D, DFF])
        self.w_up = din("w_up", [D, DFF])
        self.w_down = din("w_down", [DFF, D])
        self.w_pg = din("w_ple_gate", [D, D])
        self.w_pu = din("w_ple_up", [DPLE, D])
        self.w2 = din("rw_w2", [64, RW])
        self.a2 = din("rw_a2", [64, RW])
        self.g2 = din("rw_g2", [128, RW])
        self.pv_d = din("pv", [128, NPV])
        self.pb_d = din("pb", [128, 2048])
        self.cst_d = din("cst", [128, NCS])
        self.rot_d = din("rot", [128, 4096])
        self.out = nc.dram_tensor("out", [NT, D], F32, kind="ExternalOutput").ap()
        self.x1 = nc.dram_tensor("x1_scr", [NT, D], F32, kind="Internal").ap()
        self.xm = nc.dram_tensor("xm_scr", [NT, D], F32, kind="Internal").ap()
        P = self.P
        self.banks = [P.keep(nc.psum_tensor("ps%d" % i, [128, 512], F32)) for i in range(8)]
        self.rb = [Res("bank%d" % i, excl=True) for i in range(8)]
        self.bankb = [b[:].bitcast(BF16) for b in self.banks]
        self.uid = 0

    def T(self, name, shape, dt):
        self.uid += 1
        return Tile(self.P, "%s_%d" % (name, self.uid), shape, dt)

    def op(self, eng, fn, R=(), W=()):
        rs = [t.r if isinstance(t, Tile) else t for t in R]
        ws = [t.r if isinstance(t, Tile) else t for t in W]
        return self.P.op(eng, fn, reads=rs, writes=ws)

    def dma(self, out, in_, R=(), W=()):
        return self.op("sp", lambda e: e.dma_start(out=out, in_=in_), R, W)

    def dump(self, name, tile, ap=None):
        if name not in self.dbg:
            return
        ap = tile[:] if ap is None else ap
        shape = list(ap.shape)
        if ap.dtype != F32:
            tmp = self.T("dbg", shape, F32)
            idx = tuple(slice(None) for _ in shape)
            self.op("act", lambda e: e.activation(out=tmp[idx], in_=ap, func=AF.Copy), R=[tile], W=[tmp])
            src, srct = tmp[idx], tmp
        else:
            src, srct = ap, tile
        o = self.nc.dram_tensor(name, shape, F32, kind="ExternalOutput").ap()
        self.dbg_outs[name] = shape
        self.dma(o, src, R=[srct])

    def phase_barrier(self, mark):
        self.P.barrier(lambda e: e.dma_start(out=self.bdummy[0:1, 0:16], in_=self.cst_d[0:1, 0:16]))
        self.P.release_to(mark)

    def setup_common(self):
        self.pv = self.T("pv", [128, NPV], F32)
        self.dma(self.pv[:], self.pv_d, W=[self.pv])
        self.stage_i = 0
        self.bdummy = self.T("bdummy", [1, 16], F32)
        self.idb = self.T("idb", [128, 128], BF16)
        idf = self.T("idf", [128, 128], F32)
        self.dma(idf[:], self.cst_d[:, 0:128], W=[idf])
        self.op("dve", lambda e: e.tensor_copy(self.idb[:], idf[:]), R=[idf], W=[self.idb])
        self.mhalf = self.T("mhalf", [128, 64], F32)
        self.op("pool", lambda e: e.memset(self.mhalf[:], -0.5), W=[self.mhalf])
        self.dv = self.T("dv", [128, 32], F32)

        def derive(dst0, n, name, s1, s2):
            o = PV[name][0]
            if s2 is None:
                self.op("dve", lambda e: e.tensor_scalar(self.dv[:, dst0:dst0 + n], self.pv[:, o:o + n], s1, None, op0=ALU.mult),
                        R=[self.pv], W=[self.dv])
            else:
                self.op("dve", lambda e: e.tensor_scalar(self.dv[:, dst0:dst0 + n], self.pv[:, o:o + n], s1, s2,
                                                         op0=ALU.mult, op1=ALU.add), R=[self.pv], W=[self.dv])

        derive(0, 14, "mu", -1.0, 1.0)
        derive(14, 4, "w0", 0.5, None)
        derive(18, 4, "a0", 0.5, None)
        derive(22, 4, "k_k", -1.0, None)
        derive(26, 4, "k_a", -1.0, 1.0)

    def alloc_stage(self):
        self.stage_mark = self.P.mark()
        self.stage = [self.T("stage", [128, 1024], F32) for _ in range(6)]

    def free_stage(self):
        self.phase_barrier(self.stage_mark)

    def pvc(self, name, j=None):
        o, w = PV[name]
        if j is None:
            return self.pv[:, o:o + w]
        return self.pv[:, o + j:o + j + 1]

    def load_w(self, dst_tile, dst_ap, src_ap, ncols, scale_ap=None, scale_imm=None, rows=128, prow=0):
        st = self.stage[self.stage_i % 6]
        eng = "dve" if self.stage_i % 2 == 0 else "act"
        self.stage_i += 1
        sl = st[prow:prow + rows, 0:ncols]
        self.dma(sl, src_ap, W=[st])
        extra = [self.pv] if scale_ap is not None else []
        if scale_ap is not None and scale_imm is not None:
            self.op("dve", lambda e: e.tensor_scalar(dst_ap, sl, scale_ap, scale_imm, op0=ALU.mult, op1=ALU.mult),
                    R=[st] + extra, W=[dst_tile])
        elif eng == "dve":
            if scale_ap is not None:
                self.op("dve", lambda e: e.tensor_scalar(dst_ap, sl, scale_ap, None, op0=ALU.mult),
                        R=[st] + extra, W=[dst_tile])
            else:
                self.op("dve", lambda e: e.tensor_copy(dst_ap, sl), R=[st], W=[dst_tile])
        else:
            if scale_ap is not None:
                self.op("act", lambda e: e.activation(out=dst_ap, in_=sl, func=AF.Copy, scale=scale_ap),
                        R=[st] + extra, W=[dst_tile])
            else:
                self.op("act", lambda e: e.activation(out=dst_ap, in_=sl, func=AF.Copy), R=[st], W=[dst_tile])

    def load_w_cols(self, dst_tile, dst_fn, src, r0, c0, ncols, scale_ap=None, scale_imm=None):
        for a in range(0, ncols, 1024):
            n = min(1024, ncols - a)
            self.load_w(dst_tile, dst_fn(a, a + n), src[r0:r0 + 128, c0 + a:c0 + a + n], n, scale_ap, scale_imm)

    def norm_only(self, x_tm, nblk, xs, hnT, ss, vv, rstd, junk, bank):
        for b in range(nblk):
            self.op("act", lambda e, b=b: e.activation(out=junk[:], in_=x_tm[:, b, :], func=AF.Square,
                                                        accum_out=ss[:, b:b + 1]),
                    R=[x_tm], W=[junk, ss])
        self.op("dve", lambda e: e.tensor_scalar(vv[:, 0:nblk], ss[:, 0:nblk], 1.0 / D, EPS, op0=ALU.mult, op1=ALU.add),
                R=[ss], W=[vv])
        self.op("pool", lambda e: e.tensor_tensor(rstd[:, 0:nblk], vv[:, 0:nblk], self.mhalf[:, 0:nblk], op=ALU.pow),
                R=[vv, self.mhalf], W=[rstd])
        for b in range(nblk):
            self.op("act", lambda e, b=b: e.activation(out=xs[:, b, :], in_=x_tm[:, b, :], func=AF.Copy,
                                                        scale=rstd[:, b:b + 1]),
                    R=[x_tm, rstd], W=[xs])
        pb = self.bankb[bank]
        for b in range(nblk):
            for k in range(8):
                self.op("pe", lambda e, b=b, k=k: e.transpose(pb[:, k * 128:(k + 1) * 128],
                                                               xs[:, b, k * 128:(k + 1) * 128], self.idb[:]),
                        R=[xs, self.idb], W=[self.rb[bank]])
            self.op("dve", lambda e, b=b: e.tensor_copy(hnT[:, :, b * 128:(b + 1) * 128],
                                                         pb[:, 0:1024].rearrange("p (k t) -> p k t", k=8)),
                    R=[self.rb[bank]], W=[hnT])

    def phase2(self, src, src_res):
        TT, NB = 128, 1
        Wgu = self.T("Wgu", [128, 8, 2, DFF], BF16)
        Wd = self.T("Wd", [128, NFF, D], BF16)
        Wpg = self.T("Wpg", [128, 8, D], BF16)
        Wpu = self.T("Wpu", [128, 2, D], BF16)
        fnw = self.T("fnw", [128, D], F32)
        self.dma(fnw[:], self.pb_d[:, 1024:2048], W=[fnw])
        self.alloc_stage()
        for k in range(8):
            for gi, wsrc in enumerate((self.w_gate, self.w_up)):
                self.load_w_cols(Wgu, lambda lo, hi, k=k, gi=gi: Wgu[:, k, gi, lo:hi], wsrc, k * 128, 0, DFF,
                                 scale_ap=self.pvc("nw_ffn", k))
        for f in range(NFF):
            self.load_w(Wd, Wd[:, f, :], self.w_down[f * 128:(f + 1) * 128, :], D)
        for k in range(8):
            self.load_w(Wpg, Wpg[:, k, :], self.w_pg[k * 128:(k + 1) * 128, :], D, scale_ap=self.pvc("nw_ple", k))
        for k in range(2):
            self.load_w(Wpu, Wpu[:, k, :], self.w_pu[k * 128:(k + 1) * 128, :], D)
        self.free_stage()

        x_tm = [self.T("x2tm", [128, NB, D], F32) for _ in range(3)]
        p_tm = [self.T("p2tm", [128, NB, DPLE], F32) for _ in range(3)]
        xsA = self.T("x2sA", [128, NB, D], BF16)
        xsB = self.T("x2sB", [128, NB, D], BF16)
        hTA2 = [self.T("hfTA", [128, 8, TT], BF16) for _ in range(2)]
        hTB = self.T("hfTB", [128, 8, TT], BF16)
        junkA = self.T("junk2A", [128, D], BF16)
        junkB = self.T("junk2B", [128, D], BF16)
        stA = [self.T("stA", [128, 4], F32) for _ in range(3)]
        stB = [self.T("stB", [128, 4], F32) for _ in range(3)]
        stC = [self.T("stC", [128, 4], F32) for _ in range(3)]
        sil = [self.T("sil", [128, TT], F32) for _ in range(4)]
        hh = self.T("hh", [128, NFF, TT], BF16)
        pbf = self.T("pbf", [128, NB, DPLE], BF16)
        pT = self.T("pT", [128, 2, TT], BF16)
        th = self.T("thp", [128, 512], F32)
        tt = self.T("ttp", [128, 512], F32)
        ntile = self.NT // TT
        B = self.banks
        rb = self.rb

        def headA1(ti):
            row0 = ti * TT
            sl = ti % 3
            xt = x_tm[sl]
            rres = [src_res[row0 // 128]] if src_res is not None else []
            self.dma(xt[:], src[row0:row0 + TT, :].rearrange("(b p) d -> p b d", p=128), R=rres, W=[xt])
            self.dma(p_tm[sl][:], self.p[row0:row0 + TT, :].rearrange("(b p) d -> p b d", p=128), W=[p_tm[sl]])
            ss, vv, rstd = stA
            self.op("act", lambda e: e.activation(out=junkA[:], in_=xt[:, 0, :], func=AF.Square, accum_out=ss[:, 0:1]),
                    R=[xt], W=[junkA, ss])
            self.op("dve", lambda e: e.tensor_scalar(vv[:, 0:1], ss[:, 0:1], 1.0 / D, EPS, op0=ALU.mult, op1=ALU.add),
                    R=[ss], W=[vv])
            self.op("pool", lambda e: e.tensor_tensor(rstd[:, 0:1], vv[:, 0:1], self.mhalf[:, 0:1], op=ALU.pow),
                    R=[vv, self.mhalf], W=[rstd])
            self.op("act", lambda e: e.activation(out=xsA[:, 0, :], in_=xt[:, 0, :], func=AF.Copy, scale=rstd[:, 0:1]),
                    R=[xt, rstd], W=[xsA])

        def headA2(ti):
            hT_ = hTA2[ti % 2]
            pb = self.bankb[0]
            for k in range(8):
                self.op("pe", lambda e, k=k: e.transpose(pb[:, k * 128:(k + 1) * 128], xsA[:, 0, k * 128:(k + 1) * 128], self.idb[:]),
                        R=[xsA, self.idb], W=[rb[0]])
            self.op("dve", lambda e: e.tensor_copy(hT_[:].rearrange("p k t -> p (k t)"), pb[:, 0:1024]), R=[rb[0]], W=[hT_])

        def headA(ti):
            headA1(ti)
            headA2(ti)

        def stageA(ti):
            xt = x_tm[ti % 3]
            hTA = hTA2[ti % 2]
            for f in range(NFF):
                bank = 1 + (f % 4)
                for gi in range(2):
                    for k in range(8):
                        self.op("pe", lambda e, f=f, gi=gi, k=k, bank=bank: e.matmul(
                            B[bank][:, gi * TT:(gi + 1) * TT], Wgu[:, k, gi, f * 128:(f + 1) * 128],
                            hTA[:, k, :], start=(k == 0), stop=(k == 7)), R=[Wgu, hTA], W=[rb[bank]])
                s = sil[f % 4]
                self.op("act", lambda e, s=s, bank=bank: e.activation(out=s[:], in_=B[bank][:, 0:TT], func=AF.Silu),
                        R=[rb[bank]], W=[s])
                self.op("dve", lambda e, s=s, f=f, bank=bank: e.tensor_tensor(hh[:, f, :], s[:], B[bank][:, TT:2 * TT], op=ALU.mult),
                        R=[s, rb[bank]], W=[hh])
                if f == 5 and ti + 1 < ntile:
                    headA1(ti + 1)
                if f == 14 and ti + 1 < ntile:
                    headA2(ti + 1)
                yield
            for n in range(2):
                bank = 3 + n
                for f in range(NFF):
                    self.op("pe", lambda e, n=n, f=f, bank=bank: e.matmul(
                        B[bank][:, :], hh[:, f, :], Wd[:, f, n * 512:(n + 1) * 512],
                        start=(f == 0), stop=(f == NFF - 1)), R=[hh, Wd], W=[rb[bank]])
                self.op("dve", lambda e, n=n, bank=bank: e.tensor_tensor(
                    xt[:, 0, n * 512:(n + 1) * 512], xt[:, 0, n * 512:(n + 1) * 512], B[bank][:, :], op=ALU.add),
                    R=[xt, rb[bank]], W=[xt])
                yield

        def stageB(ti):
            row0 = ti * TT
            sl = ti % 3
            xt = x_tm[sl]
            self.norm_only(xt, NB, xsB, hTB, stB[0], stB[1], stB[2], junkB, 5)
            yield
            self.op("act", lambda e: e.activation(out=pbf[:], in_=p_tm[sl][:], func=AF.Copy), R=[p_tm[sl]], W=[pbf])
            pb5 = self.bankb[5]
            for k in range(2):
                self.op("pe", lambda e, k=k: e.transpose(pb5[:, k * 128:(k + 1) * 128], pbf[:, 0, k * 128:(k + 1) * 128], self.idb[:]),
                        R=[pbf, self.idb], W=[rb[5]])
            self.op("act", lambda e: e.activation(out=pT[:].rearrange("p k t -> p (k t)"), in_=pb5[:, 0:256], func=AF.Copy),
                    R=[rb[5]], W=[pT])
            yield
            for n in range(2):
                for k in range(8):
                    self.op("pe", lambda e, n=n, k=k: e.matmul(B[6][:, :], hTB[:, k, :], Wpg[:, k, n * 512:(n + 1) * 512],
                                                               start=(k == 0), stop=(k == 7)), R=[hTB, Wpg], W=[rb[6]])
                for k in range(2):
                    self.op("pe", lambda e, n=n, k=k: e.matmul(B[7][:, :], pT[:, k, :], Wpu[:, k, n * 512:(n + 1) * 512],
                                                               start=(k == 0), stop=(k == 1)), R=[pT, Wpu], W=[rb[7]])
                yield
                self.op("act", lambda e: e.activation(out=th[:], in_=B[6][:, :], func=AF.Tanh, scale=0.5), R=[rb[6]], W=[th])
                self.op("dve", lambda e: e.scalar_tensor_tensor(tt[:], th[:], 1.0, B[7][:, :], op0=ALU.add, op1=ALU.mult),
                        R=[th, rb[7]], W=[tt])
                self.op("dve", lambda e, n=n: e.scalar_tensor_tensor(
                    xt[:, 0, n * 512:(n + 1) * 512], tt[:], 0.5, xt[:, 0, n * 512:(n + 1) * 512],
                    op0=ALU.mult, op1=ALU.add), R=[tt, xt], W=[xt])
                yield
            ss, vv, rstd = stC
            self.op("act", lambda e: e.activation(out=junkB[:], in_=xt[:, 0, :], func=AF.Square, accum_out=ss[:, 0:1]),
                    R=[xt], W=[junkB, ss])
            self.op("dve", lambda e: e.tensor_scalar(vv[:, 0:NB], ss[:, 0:NB], 1.0 / D, EPS, op0=ALU.mult, op1=ALU.add),
                    R=[ss], W=[vv])
            self.op("pool", lambda e: e.tensor_tensor(rstd[:, 0:NB], vv[:, 0:NB], self.mhalf[:, 0:NB], op=ALU.pow),
                    R=[vv, self.mhalf], W=[rstd])
            yield
            self.op("dve", lambda e: e.scalar_tensor_tensor(xt[:, 0, :], xt[:, 0, :], rstd[:, 0:1], fnw[:],
                                                            op0=ALU.mult, op1=ALU.mult), R=[xt, rstd, fnw], W=[xt])
            self.dma(self.out[row0:row0 + TT, :].rearrange("(b p) d -> p b d", p=128), xt[:], R=[xt])

        headA(0)
        prev = None
        for ti in range(ntile):
            merge_ratio(stageA(ti), prev, 3)
            prev = stageB(ti)
        drain(prev)

    def build(self):
        self.setup_common()
        m0 = self.P.mark()
        src, src_res = self.x, None
        if "1a" in self.phases:
            self.phase1a()
            self.phase_barrier(m0)
            src, src_res = self.x1, None
        if "1b" in self.phases:
            self.phase1b(src)
            self.phase_barrier(m0)
            src, src_res = self.xm, None
        if "2" in self.phases:
            self.phase2(src, src_res)
        else:
            for r0 in range(0, self.NT, 256):
                self.dma(self.out[r0:r0 + 256, :], src[r0:r0 + 256, :])
        self.P.emit()
        return self.nc


def host_layout(inputs, n_core=NCORE, n_seq=SEQ_PER_CORE, n_tok=SEQ):
    f = lambda a: np.ascontiguousarray(np.asarray(a, dtype=np.float32))
    x = f(inputs["x"])
    p = f(inputs["p"])[0]
    cst, rot = host_consts()
    pv = np.zeros((128, NPV), np.float32)

    def putv(name, vec, ncol):
        o, w = PV[name]
        pv[:, o:o + w] = np.asarray(vec, np.float32).reshape(ncol, 128).T

    putv("nw_mix", inputs["norm_mix_w"][0], 8)
    putv("ret_nw", inputs["ret_norm_w"][0], 4)
    putv("mu", inputs["rw_mu"][0], 14)
    putv("w0", inputs["rw_w0"][0], 4)
    putv("a0", inputs["rw_a0"][0], 4)
    putv("k_k", inputs["rw_k_k"][0], 4)
    putv("k_a", inputs["rw_k_a"][0], 4)
    putv("r_k", np.asarray(inputs["rw_r_k"][0]).reshape(-1), 4)
    putv("nw_ffn", inputs["norm_ffn_w"][0], 8)
    putv("nw_ple", inputs["norm_ple_w"][0], 8)
    pb = np.concatenate([np.asarray(inputs["rw_ln_w"][0], np.float32), np.asarray(inputs["rw_ln_b"][0], np.float32),
                         np.asarray(inputs["final_norm_w"], np.float32)])
    pb = np.ascontiguousarray(np.broadcast_to(pb[None, :], (128, 2048)))
    shared = {
        "w_in": f(inputs["w_in"][0]), "w_o": f(inputs["w_o"][0]), "w_gate": f(inputs["w_gate"][0]),
        "w_up": f(inputs["w_up"][0]), "w_down": f(inputs["w_down"][0]), "w_ple_gate": f(inputs["w_ple_gate"][0]),
        "w_ple_up": f(inputs["w_ple_up"][0]), "rw_w2": f(inputs["rw_w2"][0]), "rw_a2": f(inputs["rw_a2"][0]),
        "rw_g2": f(inputs["rw_g2"][0]), "pv": pv, "pb": pb, "cst": cst, "rot": rot,
    }
    maps = []
    for c in range(n_core):
        m = dict(shared)
        xs = x[c * n_seq:(c + 1) * n_seq, :n_tok, :].reshape(n_seq * n_tok, D)
        ps = p[c * n_seq:(c + 1) * n_seq, :n_tok, :].reshape(n_seq * n_tok, DPLE)
        m["x"] = np.ascontiguousarray(xs)
        m["p"] = np.ascontiguousarray(ps)
        maps.append(m)
    return maps


def kernel(**inputs):
    b = Builder()
    nc = b.build()
    maps = host_layout(inputs)
    res = run_bass_kernel_spmd(nc, maps, core_ids=list(range(NCORE)))
    outs = [np.asarray(r["out"], dtype=np.float32).reshape(SEQ_PER_CORE, SEQ, D) for r in res.results]
    return np.concatenate(outs, axis=0)


def merge(*gens):
    gens = [g for g in gens if g is not None]
    while gens:
        for g in list(gens):
            try:
                next(g)
            except StopIteration:
                gens.remove(g)


def merge_ratio(long_gen, short_gen, r):
    la, sa = long_gen is not None, short_gen is not None
    while la or sa:
        for _ in range(r):
            if la:
                try:
                    next(long_gen)
                except StopIteration:
                    la = False
        if sa:
            try:
                next(short_gen)
            except StopIteration:
                sa = False


def drain(g):
    if g is not None:
        for _ in g:
            pass


def _phase1a(self):
    TT, NB = 512, 4
    Wa = self.T("Wa", [128, 8, 2048], BF16)
    Wo = self.T("WoT", [128, 4, D], BF16)
    rot = self.T("rot", [128, 2, 16, 128], F32)
    ca = self.T("ca", [128, CA1 - CA0], F32)
    self.dma(rot[:].rearrange("p a b c -> p (a b c)"), self.rot_d, W=[rot])
    self.dma(ca[:], self.cst_d[:, CA0:CA1], W=[ca])

    def cc(name):
        o, w = CS[name]
        return ca[:, o - CA0:o - CA0 + w]

    self.alloc_stage()
    for k in range(8):
        self.load_w_cols(Wa, lambda lo, hi, k=k: Wa[:, k, lo:hi], self.w_in, k * 128, 0, 2048,
                         scale_ap=self.pvc("nw_mix", k))
    for k in range(4):
        self.load_w(Wo, Wo[:, k, :], self.w_o[k * 128:(k + 1) * 128, :], D,
                    scale_ap=self.pvc("ret_nw", k), scale_imm=0.5)
    maskT = self.T("maskTb", [128, 128], BF16)
    self.op("dve", lambda e: e.tensor_copy(maskT[:], cc("maskT")), R=[ca], W=[maskT])
    gCdb = self.T("gCdb", [128, 512], BF16)
    self.op("dve", lambda e: e.tensor_copy(gCdb[:], cc("gCd")), R=[ca], W=[gCdb])

    x_tm = [self.T("xa", [128, NB, D], F32) for _ in range(2)]
    xs = self.T("xsa", [128, NB, D], BF16)
    hnT = [self.T("hnTa", [128, 8, TT], BF16) for _ in range(2)]
    mixT = self.T("mixTa", [128, 4, 128], BF16)
    junk = self.T("junka", [128, D], BF16)
    ss = self.T("ssa", [128, 4], F32)
    vv = self.T("vva", [128, 4], F32)
    rstd = self.T("rstda", [128, 4], F32)

    def two(name, shape, dt):
        return [self.T(name, shape, dt) for _ in range(2)]

    tA = two("tA", [128, 1024], F32)
    tB = two("tB", [128, 1024], F32)
    qrot = two("qrot", [128, 512], BF16)
    ktmp = two("ktmp", [128, 512], F32)
    khat = two("khat", [128, 512], BF16)
    qkT = two("qkT", [128, 8, 128], BF16)
    v_tm = two("v_tm", [128, 512], BF16)
    vs = two("vs", [128, 512], BF16)
    thg = two("thg", [128, 512], F32)
    sg = two("sg", [128, 512], F32)
    sT = two("sT", [128, 512], BF16)
    ysb = self.T("ysb", [128, 512], F32)
    ysq = self.T("ysq", [128, 512], F32)
    yc = self.T("yc", [128, 512], F32)
    mret = self.T("mret", [128, 512], BF16)
    R = self.T("Rst", [128, 512], F32)
    Rb = self.T("Rbf", [128, 512], BF16)
    st = self.T("hst", [128, 24], F32)
    B = self.banks
    rb = self.rb

    def stage1(seq, ti, b, xt, hn, par):
        tk = slice(b * 128, (b + 1) * 128)
        posblk = ti * 4 + b
        tA_, tB_, qrot_, ktmp_, khat_, qkT_ = tA[par], tB[par], qrot[par], ktmp[par], khat[par], qkT[par]
        v_, vs_, thg_, sg_, sT_ = v_tm[par], vs[par], thg[par], sg[par], sT[par]
        for c in range(4):
            for k in range(8):
                self.op("pe", lambda e, c=c, k=k: e.matmul(B[c][:, :], hn[:, k, tk], Wa[:, k, c * 512:(c + 1) * 512],
                                                           start=(k == 0), stop=(k == 7)), R=[hn, Wa], W=[rb[c]])
            yield
        cosb = rot[:, 0, posblk, :].unsqueeze(1).to_broadcast([128, 4, 128])
        sinv = rot[:, 1, posblk, :].rearrange("p (i two) -> p i two", two=2)
        for c in range(2):
            pv4 = B[c][:, :].rearrange("p (h e) -> p h e", h=4)
            self.op("dve", lambda e, c=c, pv4=pv4: e.tensor_tensor(
                tA_[:, c * 512:(c + 1) * 512].rearrange("p (h e) -> p h e", h=4), pv4, cosb, op=ALU.mult),
                R=[rb[c], rot], W=[tA_])
            pe2 = B[c][:, :].rearrange("p (h i two) -> p h i two", h=4, two=2)
            tb2 = tB_[:, c * 512:(c + 1) * 512].rearrange("p (h i two) -> p h i two", h=4, two=2)
            for par2 in range(2):
                self.op("dve", lambda e, par2=par2, pe2=pe2, tb2=tb2: e.tensor_tensor(
                    tb2[:, :, :, par2], pe2[:, :, :, 1 - par2],
                    sinv[:, :, par2].unsqueeze(1).to_broadcast([128, 4, 64]), op=ALU.mult),
                    R=[rb[c], rot], W=[tB_])
            yield
        self.op("act", lambda e: e.activation(out=v_[:], in_=B[2][:, :], func=AF.Copy), R=[rb[2]], W=[v_])
        self.op("act", lambda e: e.activation(out=thg_[:], in_=B[3][:, :], func=AF.Tanh, scale=0.5), R=[rb[3]], W=[thg_])
        self.op("dve", lambda e: e.scalar_tensor_tensor(sg_[:], thg_[:], 1.0, B[3][:, :], op0=ALU.add, op1=ALU.mult),
                R=[thg_, rb[3]], W=[sg_])
        self.op("pool", lambda e: e.tensor_tensor(vs_[:], v_[:], gCdb[:], op=ALU.mult), R=[v_, gCdb], W=[vs_])
        yield
        self.op("dve", lambda e: e.tensor_tensor(qrot_[:], tA_[:, 0:512], tB_[:, 0:512], op=ALU.add), R=[tA_, tB_], W=[qrot_])
        self.op("dve", lambda e: e.tensor_tensor(ktmp_[:], tA_[:, 512:1024], tB_[:, 512:1024], op=ALU.add), R=[tA_, tB_], W=[ktmp_])
        self.op("dve", lambda e: e.tensor_tensor(
            khat_[:].rearrange("p (h e) -> p h e", h=4), ktmp_[:].rearrange("p (h e) -> p h e", h=4),
            cc("ghat").unsqueeze(2).to_broadcast([128, 4, 128]), op=ALU.mult), R=[ktmp_, ca], W=[khat_])
        yield
        pb = self.bankb[4]
        for h in range(4):
            self.op("pe", lambda e, h=h: e.transpose(pb[:, h * 128:(h + 1) * 128], qrot_[:, h * 128:(h + 1) * 128], self.idb[:]),
                    R=[qrot_, self.idb], W=[rb[4]])
        for h in range(4):
            self.op("pe", lambda e, h=h: e.transpose(pb[:, (4 + h) * 128:(5 + h) * 128], khat_[:, h * 128:(h + 1) * 128], self.idb[:]),
                    R=[khat_, self.idb], W=[rb[4]])
        self.op("act", lambda e: e.activation(out=qkT_[:].rearrange("p a t -> p (a t)"), in_=pb[:, 0:1024], func=AF.Copy),
                R=[rb[4]], W=[qkT_])
        yield
        for h in range(4):
            self.op("pe", lambda e, h=h: e.matmul(B[5][:, h * 128:(h + 1) * 128], qkT_[:, 4 + h, :], qkT_[:, h, :],
                                                  start=True, stop=True), R=[qkT_], W=[rb[5]])
        self.op("dve", lambda e: e.tensor_tensor(
            sT_[:].rearrange("p (h c) -> p h c", h=4), B[5][:, :].rearrange("p (h c) -> p h c", h=4),
            maskT[:].unsqueeze(1).to_broadcast([128, 4, 128]), op=ALU.mult), R=[rb[5], maskT], W=[sT_])
        yield

    def stage2(seq, ti, b, xt, par, last):
        tk = slice(b * 128, (b + 1) * 128)
        khat_, qkT_, v_, vs_, sg_, sT_ = khat[par], qkT[par], v_tm[par], vs[par], sg[par], sT[par]
        for h in range(4):
            hs = slice(h * 128, (h + 1) * 128)
            self.op("pe", lambda e, hs=hs: e.matmul(B[6][:, hs], sT_[:, hs], v_[:, hs], start=True, stop=False),
                    R=[sT_, v_], W=[rb[6]])
            self.op("pe", lambda e, hs=hs, h=h: e.matmul(B[6][:, hs], qkT_[:, h, :], Rb[:, hs], start=False, stop=True),
                    R=[qkT_, Rb], W=[rb[6]])
        for h in range(4):
            hs = slice(h * 128, (h + 1) * 128)
            self.op("pe", lambda e, hs=hs: e.matmul(B[7][:, hs], khat_[:, hs], vs_[:, hs], start=True, stop=True),
                    R=[khat_, vs_], W=[rb[7]])
        yield
        self.op("act", lambda e: e.activation(out=ysb[:], in_=B[6][:, :], func=AF.Copy), R=[rb[6]], W=[ysb])
        self.op("dve", lambda e: e.tensor_tensor(R[:], R[:], cc("gC"), op=ALU.mult), R=[R, ca], W=[R])
        self.op("dve", lambda e: e.tensor_tensor(R[:], R[:], B[7][:, :], op=ALU.add), R=[R, rb[7]], W=[R])
        self.op("act", lambda e: e.activation(out=Rb[:], in_=R[:], func=AF.Copy), R=[R], W=[Rb])
        yield
        self.op("dve", lambda e: e.tensor_reduce(st[:, 0:4], ysb[:].rearrange("p (h e) -> p h e", h=4), axis=AX.X, op=ALU.add),
                R=[ysb], W=[st])
        self.op("act", lambda e: e.activation(out=ysq[:], in_=ysb[:], func=AF.Square), R=[ysb], W=[ysq])
        self.op("dve", lambda e: e.tensor_reduce(st[:, 4:8], ysq[:].rearrange("p (h e) -> p h e", h=4), axis=AX.X, op=ALU.add),
                R=[ysq], W=[st])
        yield
        self.op("dve", lambda e: e.tensor_scalar(st[:, 8:12], st[:, 0:4], 1.0 / 128, None, op0=ALU.mult), R=[st], W=[st])
        self.op("dve", lambda e: e.tensor_tensor(st[:, 12:16], st[:, 8:12], st[:, 8:12], op=ALU.mult), R=[st], W=[st])
        self.op("dve", lambda e: e.scalar_tensor_tensor(st[:, 16:20], st[:, 4:8], 1.0 / 128, st[:, 12:16],
                                                        op0=ALU.mult, op1=ALU.subtract), R=[st], W=[st])
        self.op("dve", lambda e: e.tensor_tensor(st[:, 16:20], st[:, 16:20], cc("epsr"), op=ALU.add), R=[st, ca], W=[st])
        self.op("pool", lambda e: e.tensor_tensor(st[:, 20:24], st[:, 16:20], self.mhalf[:, 0:4], op=ALU.pow),
                R=[st, self.mhalf], W=[st])
        yield
        for h in range(4):
            hs = slice(h * 128, (h + 1) * 128)
            self.op("dve", lambda e, h=h, hs=hs: e.tensor_scalar(yc[:, hs], ysb[:, hs], st[:, 8 + h:9 + h], st[:, 20 + h:21 + h],
                                                                 op0=ALU.subtract, op1=ALU.mult), R=[ysb, st], W=[yc])
        self.op("dve", lambda e: e.tensor_tensor(mret[:], yc[:], sg_[:], op=ALU.mult), R=[yc, sg_], W=[mret])
        if seq == 0 and ti == 0 and b == 0:
            self.dump("d_mret", mret)
            self.dump("d_ysb", ysb)
        yield
        pb = self.bankb[6]
        for h in range(4):
            self.op("pe", lambda e, h=h: e.transpose(pb[:, h * 128:(h + 1) * 128], mret[:, h * 128:(h + 1) * 128], self.idb[:]),
                    R=[mret, self.idb], W=[rb[6]])
        self.op("act", lambda e: e.activation(out=mixT[:].rearrange("p k t -> p (k t)"), in_=pb[:, 0:512], func=AF.Copy),
                R=[rb[6]], W=[mixT])
        yield
        for n in range(2):
            bank = 7 - n
            for k in range(4):
                self.op("pe", lambda e, n=n, k=k, bank=bank: e.matmul(B[bank][:, :], mixT[:, k, :], Wo[:, k, n * 512:(n + 1) * 512],
                                                                      start=(k == 0), stop=(k == 3)), R=[mixT, Wo], W=[rb[bank]])
            self.op("dve", lambda e, n=n, bank=bank: e.tensor_tensor(xt[:, b, n * 512:(n + 1) * 512], xt[:, b, n * 512:(n + 1) * 512],
                                                                     B[bank][:, :], op=ALU.add), R=[xt, rb[bank]], W=[xt])
            yield
        if last:
            row0 = seq * self.n_tok + ti * TT
            self.dma(self.x1[row0:row0 + TT, :].rearrange("(b p) d -> p b d", p=128), xt[:], R=[xt])

    def head(seq, ti, xt, hn):
        row0 = seq * self.n_tok + ti * TT
        self.dma(xt[:], self.x[row0:row0 + TT, :].rearrange("(b p) d -> p b d", p=128), W=[xt])
        self.norm_only(xt, NB, xs, hn, ss, vv, rstd, junk, 4)
        yield

    def s1_stream(seq, ti, b, xt, hn, par):
        if b == 0:
            yield from head(seq, ti, xt, hn)
        yield from stage1(seq, ti, b, xt, hn, par)

    def s2_stream(seq, ti, b, xt, par):
        if ti == 0 and b == 0:
            self.op("dve", lambda e: e.memset(R[:], 0.0), W=[R])
            self.op("dve", lambda e: e.memset(Rb[:], 0.0), W=[Rb])
        yield from stage2(seq, ti, b, xt, par, b == NB - 1)

    units = []
    cnt = 0
    for seq in range(self.n_seq):
        for ti in range(self.n_tok // TT):
            for b in range(NB):
                units.append((seq, ti, b, x_tm[cnt % 2], hnT[cnt % 2]))
            cnt += 1
    prev = None
    for i, (seq, ti, b, xt, hn) in enumerate(units):
        par = i % 2
        merge(prev, s1_stream(seq, ti, b, xt, hn, par))
        prev = s2_stream(seq, ti, b, xt, par)
    drain(prev)


Builder.phase1a = _phase1a


def interleave(*gens):
    gens = [g for g in gens if g is not None]
    while gens:
        for g in list(gens):
            try:
                next(g)
            except StopIteration:
                gens.remove(g)
        yield


def _phase1b(self, src1):
    TT, NB, NCH = 256, 2, 4
    Wb = self.T("Wb", [128, 8, 1792], BF16)
    Wo = self.T("WoB", [128, 4, D], BF16)
    W2p = self.T("W2p", [128, RW], BF16)
    A2p = self.T("A2p", [128, RW], BF16)
    G2 = self.T("G2", [128, RW], BF16)
    lnw = self.T("lnw", [128, RW], F32)
    lnb = self.T("lnb", [128, RW], F32)
    cb = self.T("cb", [128, CB1 - CB0], F32)
    self.dma(cb[:], self.cst_d[:, CB0:CB1], W=[cb])
    self.dma(lnw[:], self.pb_d[:, 0:512], W=[lnw])
    self.dma(lnb[:], self.pb_d[:, 512:1024], W=[lnb])

    def cc(name):
        o, w = CS[name]
        return cb[:, o - CB0:o - CB0 + w]

    bonesb = self.T("bonesb", [128, 128], BF16)
    sel2b = self.T("sel2b", [128, 2], BF16)
    self.alloc_stage()
    for k in range(8):
        self.load_w_cols(Wb, lambda lo, hi, k=k: Wb[:, k, lo:hi], self.w_in, k * 128, 2048, 1792,
                         scale_ap=self.pvc("nw_mix", k))
    for k in range(4):
        self.load_w(Wo, Wo[:, k, :], self.w_o[512 + k * 128:512 + (k + 1) * 128, :], D)
    self.op("dve", lambda e: e.memset(W2p[:], 0.0), W=[W2p])
    self.op("dve", lambda e: e.memset(A2p[:], 0.0), W=[A2p])
    self.load_w(W2p, W2p[0:64, :], self.w2, RW, rows=64, prow=0)
    self.load_w(A2p, A2p[64:128, :], self.a2, RW, rows=64, prow=64)
    self.load_w(G2, G2[:, :], self.g2, RW)
    self.op("dve", lambda e: e.tensor_copy(bonesb[:], cc("bones")), R=[cb], W=[bonesb])
    self.op("dve", lambda e: e.tensor_copy(sel2b[:], cc("sel2")), R=[cb], W=[sel2b])
    self.free_stage()

    def two(name, shape, dt):
        return [self.T(name, shape, dt) for _ in range(2)]

    xt = self.T("xb", [128, NB, D], F32)
    x1t = two("x1b", [128, NB, D], F32)
    xs = self.T("xsb", [128, NB, D], BF16)
    hnT = self.T("hnTb", [128, 8, TT], BF16)
    mixT = self.T("mixTb", [128, 4, 128], BF16)
    junk = self.T("junkb", [128, D], BF16)
    ss = self.T("ssb", [128, 4], F32)
    vv = self.T("vvb", [128, 4], F32)
    rstd = self.T("rstdb", [128, 4], F32)
    carry = self.T("carry", [128, 14], F32)
    pm = two("pm", [128, TT], F32)
    h12 = self.T("h12", [128, TT], F32)
    h13 = self.T("h13", [128, TT], F32)
    lo1 = self.T("lo1", [128, TT], BF16)
    lo2 = self.T("lo2", [128, TT], BF16)
    f32names = ["hr", "hk", "lw", "alpha", "cum", "cx", "chh", "ep", "en", "rcp", "kf", "bq"]
    t = {n: self.T(n, [128, TT], F32) for n in f32names}
    bfnames = ["hv", "kk2", "bT", "kT", "bhT", "khT", "rkT"]
    tb = {n: self.T(n, [128, TT], BF16) for n in bfnames}
    ARt = two("ARt", [128, 4, 2, TT], BF16)
    bh_tm = two("bh_tm", [128, NB, RW], BF16)
    kh_tm = two("kh_tm", [128, NB, RW], BF16)
    V_tm = two("V_tm", [128, NB, RW], BF16)
    AT = [[self.T("AT", [128, 4, 512], BF16) for _ in range(4)] for _ in range(2)]
    N1q = two("N1q", [128, 4, 128], BF16)
    Nc = two("Nc", [128, 4, 128], BF16)
    Mc = two("Mc", [128, 4, 128], BF16)
    Qp = two("Qp", [128, 4, 128], BF16)
    Qfin = two("Qfin", [128, 16, 128], BF16)
    gC_all = two("gC_all", [128, NCH, 4], F32)
    g_sb = two("g_sb", [128, NB, RW], F32)
    bs_sb = two("bs_sb", [128, 16], F32)
    Xb = two("Xb", [128, RW], BF16)
    Ub = two("Ub", [128, RW], BF16)
    y_rw = self.T("y_rw", [128, NB, RW], F32)
    S = self.T("Sst", [128, RW], F32)
    tmpS = self.T("tmpS", [128, RW], F32)
    SBD = self.T("SBD", [128, RW], BF16)
    ysq = self.T("ysqb", [128, RW], F32)
    t1 = self.T("t1", [128, RW], F32)
    t2 = self.T("t2", [128, RW], F32)
    mrw = self.T("mrw", [128, RW], BF16)
    hst = self.T("hstb", [128, 56], F32)
    B = self.banks
    rb = self.rb
    BS0 = 384
    for z in Xb + Ub:
        self.op("dve", lambda e, z=z: e.memset(z[:], 0.0), W=[z])

    tsn = [0]

    def tshift(ps, rbank, ci, out_ap, out_tile):
        mu = self.pvc("mu", ci)
        omu = self.dv[:, ci:ci + 1]
        pm_ = pm[tsn[0] % 2]
        tsn[0] += 1
        self.op("act", lambda e: e.activation(out=pm_[:], in_=ps, func=AF.Copy, scale=mu), R=[rbank, self.pv], W=[pm_])
        self.op("dve", lambda e: e.scalar_tensor_tensor(out_ap[:, 1:TT], ps[:, 1:TT], omu, pm_[:, 0:TT - 1],
                                                        op0=ALU.mult, op1=ALU.add), R=[rbank, pm_, self.dv], W=[out_tile])
        self.op("dve", lambda e: e.scalar_tensor_tensor(out_ap[:, 0:1], ps[:, 0:1], omu, carry[:, ci:ci + 1],
                                                        op0=ALU.mult, op1=ALU.add), R=[rbank, carry, self.dv], W=[out_tile])
        self.op("act", lambda e: e.activation(out=carry[:, ci:ci + 1], in_=pm_[:, TT - 1:TT], func=AF.Copy),
                R=[pm_], W=[carry])

    def proj_fm(bank, c0, ci):
        for k in range(8):
            self.op("pe", lambda e, k=k: e.matmul(B[bank][:, c0:c0 + TT], Wb[:, k, ci * 128:(ci + 1) * 128], hnT[:, k, :],
                                                  start=(k == 0), stop=(k == 7)), R=[Wb, hnT], W=[rb[bank]])

    def head(seq, ti, p):
        row0 = seq * self.n_tok + ti * TT
        x1 = x1t[p]
        if ti == 0:
            self.op("dve", lambda e: e.memset(carry[:], 0.0), W=[carry])
        self.dma(xt[:], self.x[row0:row0 + TT, :].rearrange("(b p) d -> p b d", p=128), W=[xt])
        self.dma(x1[:], src1[row0:row0 + TT, :].rearrange("(b p) d -> p b d", p=128), W=[x1])
        self.norm_only(xt, NB, xs, hnT, ss, vv, rstd, junk, 0)
        yield
        proj_fm(2, 0, 12)
        proj_fm(2, TT, 13)
        yield
        tshift(B[2][:, 0:TT], rb[2], 12, h12[:], h12)
        tshift(B[2][:, TT:2 * TT], rb[2], 13, h13[:], h13)
        yield
        self.op("act", lambda e: e.activation(out=lo1[0:64, :], in_=h12[0:64, :], func=AF.Tanh), R=[h12], W=[lo1])
        self.op("act", lambda e: e.activation(out=lo1[64:128, :], in_=h12[64:128, :], func=AF.Copy), R=[h12], W=[lo1])
        self.op("act", lambda e: e.activation(out=h13[:], in_=h13[:], func=AF.Tanh, scale=0.5), R=[h13], W=[h13])
        self.op("dve", lambda e: e.tensor_scalar(lo2[:], h13[:], 0.5, 0.5, op0=ALU.mult, op1=ALU.add), R=[h13], W=[lo2])
        yield
        for blk in range(NB):
            self.op("pe", lambda e, blk=blk: e.matmul(B[3][:, :], lo2[:, blk * 128:(blk + 1) * 128], G2[:], start=True, stop=True),
                    R=[lo2, G2], W=[rb[3]])
            self.op("act", lambda e, blk=blk: e.activation(out=g_sb[p][:, blk, :], in_=B[3][:, :], func=AF.Copy),
                    R=[rb[3]], W=[g_sb[p]])
            yield

    def prepA(j, p, first):
        AT_ = AT[p][j]
        ARt_ = ARt[p]
        js = slice(j * 128, (j + 1) * 128)
        proj_fm(2, 0, j)
        proj_fm(2, TT, 4 + j)
        proj_fm(3, 0, 8 + j)
        yield
        tshift(B[2][:, 0:TT], rb[2], j, t["hr"][:], t["hr"])
        yield
        tshift(B[2][:, TT:2 * TT], rb[2], 4 + j, t["hk"][:], t["hk"])
        yield
        tshift(B[3][:, 0:TT], rb[3], 8 + j, tb["hv"][:], tb["hv"])
        self.op("pe", lambda e: e.matmul(B[0][:, 0:TT], W2p[:, js], lo1[:], start=True, stop=True), R=[W2p, lo1], W=[rb[0]])
        self.op("pe", lambda e: e.matmul(B[0][:, TT:2 * TT], A2p[:, js], lo1[:], start=True, stop=True), R=[A2p, lo1], W=[rb[0]])
        yield
        hw0 = self.dv[:, 14 + j:15 + j]
        ha0 = self.dv[:, 18 + j:19 + j]
        nkk = self.dv[:, 22 + j:23 + j]
        omka = self.dv[:, 26 + j:27 + j]
        k_k = self.pvc("k_k", j)
        k_a = self.pvc("k_a", j)
        r_k = self.pvc("r_k", j)
        lw, alpha, cum, cx, chh, ep, en, rcp, kf, bq = (t[n] for n in ("lw", "alpha", "cum", "cx", "chh", "ep", "en", "rcp", "kf", "bq"))
        self.op("act", lambda e: e.activation(out=lw[:], in_=B[0][:, 0:TT], func=AF.Tanh, bias=hw0, scale=0.5),
                R=[rb[0], self.dv], W=[lw])
        self.op("act", lambda e: e.activation(out=alpha[:], in_=B[0][:, TT:2 * TT], func=AF.Tanh, bias=ha0, scale=0.5),
                R=[rb[0], self.dv], W=[alpha])
        self.op("act", lambda e: e.activation(out=tb["kk2"][:], in_=t["hk"][:], func=AF.Square, scale=k_k),
                R=[t["hk"], self.pv], W=[tb["kk2"]])
        self.op("pe", lambda e: e.matmul(B[3][:, TT:2 * TT], bonesb[:], tb["kk2"][:], start=True, stop=True),
                R=[bonesb, tb["kk2"]], W=[rb[3]])
        yield
        self.op("dve", lambda e: e.tensor_scalar(lw[:], lw[:], 1.0, -0.5 * C_DEC, op0=ALU.add, op1=ALU.mult), R=[lw], W=[lw])
        self.op("dve", lambda e: e.tensor_scalar(alpha[:], alpha[:], 0.5, 0.5, op0=ALU.mult, op1=ALU.add), R=[alpha], W=[alpha])
        self.op("dve", lambda e: e.tensor_tensor_scan(cum[:], cc("scanmask"), lw[:], 0.0, op0=ALU.mult, op1=ALU.add),
                R=[cb, lw], W=[cum])
        yield
        self.op("dve", lambda e: e.tensor_tensor(cx[:], cum[:], lw[:], op=ALU.subtract), R=[cum, lw], W=[cx])
        cumC = cum[:].rearrange("p (c s) -> p c s", s=64)[:, :, 63:64].to_broadcast([128, NCH, 64])
        self.op("dve", lambda e: e.tensor_tensor(chh[:].rearrange("p (c s) -> p c s", s=64), cumC,
                                                 cum[:].rearrange("p (c s) -> p c s", s=64), op=ALU.subtract), R=[cum], W=[chh])
        self.op("act", lambda e: e.activation(out=ep[:], in_=cum[:], func=AF.Exp), R=[cum], W=[ep])
        self.op("act", lambda e: e.activation(out=en[:], in_=cum[:], func=AF.Exp, scale=-1.0), R=[cum], W=[en])
        self.op("act", lambda e: e.activation(out=cx[:], in_=cx[:], func=AF.Exp), R=[cx], W=[cx])
        self.op("act", lambda e: e.activation(out=chh[:], in_=chh[:], func=AF.Exp), R=[chh], W=[chh])
        self.op("act", lambda e: e.activation(out=gC_all[p][:, :, j], in_=ep[:].rearrange("p (c s) -> p c s", s=64)[:, :, 63],
                                               func=AF.Copy), R=[ep], W=[gC_all[p]])
        yield
        self.op("dve", lambda e: e.tensor_scalar(rcp[:], B[3][:, TT:2 * TT], 1e-24, None, op0=ALU.max), R=[rb[3]], W=[rcp])
        self.op("dve", lambda e: e.reciprocal(rcp[:], rcp[:]), R=[rcp], W=[rcp])
        yield
        self.op("dve", lambda e: e.tensor_scalar(kf[:], alpha[:], k_a, omka, op0=ALU.mult, op1=ALU.add),
                R=[alpha, self.pv, self.dv], W=[kf])
        self.op("pool", lambda e: e.tensor_tensor(kf[:], t["hk"][:], kf[:], op=ALU.mult), R=[t["hk"], kf], W=[kf])
        self.op("dve", lambda e: e.scalar_tensor_tensor(bq[:], t["hk"][:], k_k, alpha[:], op0=ALU.mult, op1=ALU.mult),
                R=[t["hk"], alpha, self.pv], W=[bq])
        self.op("pool", lambda e: e.tensor_tensor(bq[:], bq[:], rcp[:], op=ALU.mult), R=[bq, rcp], W=[bq])
        yield
        self.op("dve", lambda e: e.scalar_tensor_tensor(ARt_[:, j, 0, :], t["hk"][:], nkk, cx[:], op0=ALU.mult, op1=ALU.mult),
                R=[t["hk"], cx, self.dv], W=[ARt_])
        self.op("pool", lambda e: e.tensor_tensor(ARt_[:, j, 1, :], t["hr"][:], ep[:], op=ALU.mult), R=[t["hr"], ep], W=[ARt_])
        self.op("dve", lambda e: e.tensor_tensor(tb["bT"][:], bq[:], en[:], op=ALU.mult), R=[bq, en], W=[tb["bT"]])
        self.op("dve", lambda e: e.tensor_tensor(tb["kT"][:], kf[:], en[:], op=ALU.mult), R=[kf, en], W=[tb["kT"]])
        yield
        self.op("pool", lambda e: e.tensor_tensor(tb["bhT"][:], bq[:], chh[:], op=ALU.mult), R=[bq, chh], W=[tb["bhT"]])
        self.op("pool", lambda e: e.tensor_tensor(tb["khT"][:], kf[:], chh[:], op=ALU.mult), R=[kf, chh], W=[tb["khT"]])
        self.op("dve", lambda e: e.scalar_tensor_tensor(tb["rkT"][:], t["hr"][:], r_k, kf[:], op0=ALU.mult, op1=ALU.mult),
                R=[t["hr"], kf, self.pv], W=[tb["rkT"]])
        if first and j == 0:
            for nm in ("lw", "alpha", "cum", "kf", "bq", "hr", "hk"):
                self.dump("d_" + nm, t[nm])
            self.dump("d_hv", tb["hv"])
        yield
        for hb in range(2):
            hr_ = slice(hb * 64, (hb + 1) * 64)
            for blk in range(NB):
                u = hb * 2 + blk
                bank = u % 2
                tk = slice(blk * 128, (blk + 1) * 128)
                self.op("pe", lambda e, bank=bank, hr_=hr_, tk=tk: e.matmul(B[bank][:, 0:256], tb["bT"][hr_, tk], ARt_[hr_, j, :, tk],
                                                                            start=True, stop=True), R=[tb["bT"], ARt_], W=[rb[bank]])
                self.op("pe", lambda e, bank=bank, hr_=hr_, tk=tk: e.matmul(B[bank][:, 256:512], tb["kT"][hr_, tk], ARt_[hr_, j, :, tk],
                                                                            start=True, stop=True), R=[tb["kT"], ARt_], W=[rb[bank]])
                self.op("dve", lambda e, bank=bank, u=u: e.tensor_tensor(AT_[:, u, :], B[bank][:, :], cc("mask4"), op=ALU.mult),
                        R=[rb[bank], cb], W=[AT_])
                yield
        pb = self.bankb[1]
        for qi, srcT in enumerate((tb["bhT"], tb["khT"], tb["hv"])):
            for blk in range(NB):
                i = qi * 2 + blk
                self.op("pe", lambda e, i=i, blk=blk, srcT=srcT: e.transpose(pb[:, i * 128:(i + 1) * 128],
                                                                             srcT[:, blk * 128:(blk + 1) * 128], self.idb[:]),
                        R=[srcT, self.idb], W=[rb[1]])
        for blk in range(NB):
            c0 = BS0 + blk * 2
            self.op("pe", lambda e, blk=blk, c0=c0: e.matmul(B[1][:, c0:c0 + 2], tb["rkT"][:, blk * 128:(blk + 1) * 128], sel2b[:],
                                                             start=True, stop=True), R=[tb["rkT"], sel2b], W=[rb[1]])
        for qi, dst in enumerate((bh_tm[p], kh_tm[p], V_tm[p])):
            self.op("act", lambda e, qi=qi, dst=dst: e.activation(
                out=dst[:, :, js], in_=pb[:, qi * 256:(qi + 1) * 256].rearrange("p (b c) -> p b c", b=2), func=AF.Copy),
                R=[rb[1]], W=[dst])
        self.op("dve", lambda e: e.tensor_copy(
            bs_sb[p][:].rearrange("p (b h) -> p b h", b=2)[:, :, 2 * j:2 * j + 2],
            B[1][:, BS0:BS0 + 4].rearrange("p (b h) -> p b h", b=2)), R=[rb[1]], W=[bs_sb[p]])
        yield
        for u in range(4):
            self.op("pe", lambda e, u=u: e.transpose(pb[:, u * 128:(u + 1) * 128], AT_[:, u, 0:128], self.idb[:]),
                    R=[AT_, self.idb], W=[rb[1]])
        self.op("act", lambda e: e.activation(out=N1q[j % 2][:].rearrange("p u t -> p (u t)"), in_=pb[:, 0:512], func=AF.Copy),
                R=[rb[1]], W=[N1q[j % 2]])
        yield

    def neumann(j, p):
        AT_ = AT[p][j]
        N1_ = N1q[j % 2]
        self.op("dve", lambda e: e.tensor_tensor(Qp[0][:], AT_[:, :, 0:128],
                                                 self.idb[:].unsqueeze(1).to_broadcast([128, 4, 128]), op=ALU.add),
                R=[AT_, self.idb], W=[Qp[0]])
        yield
        Mprev = lambda u: AT_[:, u, 0:128]
        Nprev = lambda u: N1_[:, u, :]
        Mpt, Npt = AT_, N1_
        qi = 0
        for lev in range(1, 6):
            Ncur, Mcur = Nc[lev % 2], Mc[lev % 2]
            for u in range(4):
                self.op("pe", lambda e, u=u, Mprev=Mprev, Nprev=Nprev: e.matmul(B[4][:, u * 128:(u + 1) * 128], Mprev(u), Nprev(u),
                                                                              start=True, stop=True), R=[Mpt, Npt], W=[rb[4]])
            if lev < 5:
                for u in range(4):
                    self.op("pe", lambda e, u=u, Mprev=Mprev, Nprev=Nprev: e.matmul(B[5][:, u * 128:(u + 1) * 128], Nprev(u), Mprev(u),
                                                                                  start=True, stop=True), R=[Mpt, Npt], W=[rb[5]])
            yield
            self.op("act", lambda e, Ncur=Ncur: e.activation(out=Ncur[:].rearrange("p u t -> p (u t)"), in_=B[4][:, :], func=AF.Copy),
                    R=[rb[4]], W=[Ncur])
            if lev < 5:
                self.op("act", lambda e, Mcur=Mcur: e.activation(out=Mcur[:].rearrange("p u t -> p (u t)"), in_=B[5][:, :], func=AF.Copy),
                        R=[rb[5]], W=[Mcur])
            yield
            Qprev = Qp[qi]
            for u in range(4):
                self.op("pe", lambda e, u=u, Ncur=Ncur, Qprev=Qprev: e.matmul(B[4][:, u * 128:(u + 1) * 128], Ncur[:, u, :], Qprev[:, u, :],
                                                                            start=True, stop=True), R=[Ncur, Qprev], W=[rb[4]])
            yield
            if lev < 5:
                Qn = Qp[1 - qi]
                self.op("dve", lambda e, Qn=Qn, Qprev=Qprev: e.tensor_tensor(Qn[:].rearrange("p u t -> p (u t)"),
                                                                             Qprev[:].rearrange("p u t -> p (u t)"), B[4][:, :], op=ALU.add),
                        R=[Qprev, rb[4]], W=[Qn])
                qi = 1 - qi
            else:
                self.op("dve", lambda e, Qprev=Qprev: e.tensor_tensor(Qfin[p][:, j * 4:(j + 1) * 4, :],
                                                                      Qprev[:], B[4][:, :].rearrange("p (u t) -> p u t", u=4), op=ALU.add),
                        R=[Qprev, rb[4]], W=[Qfin[p]])
            yield
            Mprev = lambda u, Mcur=Mcur: Mcur[:, u, :]
            Nprev = lambda u, Ncur=Ncur: Ncur[:, u, :]
            Mpt, Npt = Mcur, Ncur

    def chain(c, p):
        blk, half = c // 2, c % 2
        hs = slice(half * 64, half * 64 + 64)
        tcs = slice(c * 64, (c + 1) * 64)
        ARt_, Vt, Qf, Xh, Uh = ARt[p], V_tm[p], Qfin[p], Xb[half], Ub[half]
        for j in range(4):
            js = slice(j * 128, (j + 1) * 128)
            self.op("pe", lambda e, j=j, js=js: e.matmul(B[6][hs, js], ARt_[:, j, 0, tcs], SBD[:, js], start=True, stop=False),
                    R=[ARt_, SBD], W=[rb[6]])
            for hb in range(2):
                u = hb * 2 + blk
                cs = slice(j * 128 + hb * 64, j * 128 + hb * 64 + 64)
                self.op("pe", lambda e, j=j, u=u, cs=cs, hb=hb: e.matmul(
                    B[6][hs, cs], AT[p][j][:, u, 256 + half * 64:256 + half * 64 + 64], Vt[:, blk, cs],
                    start=False, stop=(hb == 1)), R=[AT[p][j], Vt], W=[rb[6]])
        yield
        self.op("act", lambda e: e.activation(out=Xh[hs, :], in_=B[6][hs, :], func=AF.Copy), R=[rb[6]], W=[Xh])
        yield
        for j in range(4):
            for hb in range(2):
                u = hb * 2 + blk
                cs = slice(j * 128 + hb * 64, j * 128 + hb * 64 + 64)
                self.op("pe", lambda e, j=j, u=u, cs=cs: e.matmul(B[6][hs, cs], Qf[:, j * 4 + u, half * 64:half * 64 + 64], Xh[:, cs],
                                                                  start=True, stop=True), R=[Qf, Xh], W=[rb[6]])
        yield
        self.op("dve", lambda e: e.tensor_copy(Uh[hs, :], B[6][hs, :]), R=[rb[6]], W=[Uh])
        yield
        for j in range(4):
            js = slice(j * 128, (j + 1) * 128)
            self.op("pe", lambda e, js=js: e.matmul(B[6][:, js], bh_tm[p][hs, blk, js], Uh[hs, js], start=True, stop=False),
                    R=[bh_tm[p], Uh], W=[rb[6]])
            self.op("pe", lambda e, js=js: e.matmul(B[6][:, js], kh_tm[p][hs, blk, js], Vt[hs, blk, js], start=False, stop=True),
                    R=[kh_tm[p], Vt], W=[rb[6]])
        for j in range(4):
            js = slice(j * 128, (j + 1) * 128)
            self.op("pe", lambda e, j=j, js=js: e.matmul(B[7][hs, js], ARt_[:, j, 1, tcs], SBD[:, js], start=True, stop=False),
                    R=[ARt_, SBD], W=[rb[7]])
            for hb in range(2):
                u = hb * 2 + blk
                cs = slice(j * 128 + hb * 64, j * 128 + hb * 64 + 64)
                self.op("pe", lambda e, j=j, u=u, cs=cs: e.matmul(
                    B[7][hs, cs], AT[p][j][:, u, 128 + half * 64:128 + half * 64 + 64], Uh[:, cs], start=False, stop=False),
                    R=[AT[p][j], Uh], W=[rb[7]])
                self.op("pe", lambda e, j=j, u=u, cs=cs, hb=hb: e.matmul(
                    B[7][hs, cs], AT[p][j][:, u, 384 + half * 64:384 + half * 64 + 64], Vt[:, blk, cs],
                    start=False, stop=(hb == 1)), R=[AT[p][j], Vt], W=[rb[7]])
        yield
        self.op("dve", lambda e: e.tensor_tensor(tmpS[:], B[6][:, :], cc("bd4"), op=ALU.mult), R=[rb[6], cb], W=[tmpS])
        self.op("dve", lambda e: e.tensor_tensor(S[:].rearrange("p (j v) -> p j v", j=4), S[:].rearrange("p (j v) -> p j v", j=4),
                                                 gC_all[p][:, c, :].unsqueeze(2).to_broadcast([128, 4, 128]), op=ALU.mult),
                R=[S, gC_all[p]], W=[S])
        yield
        self.op("dve", lambda e: e.tensor_tensor(S[:], S[:], tmpS[:], op=ALU.add), R=[S, tmpS], W=[S])
        self.op("act", lambda e: e.activation(out=y_rw[hs, blk, :], in_=B[7][hs, :], func=AF.Copy), R=[rb[7]], W=[y_rw])
        yield
        self.op("act", lambda e: e.activation(out=SBD[:], in_=S[:], func=AF.Copy), R=[S], W=[SBD])
        yield

    def finish(blk, p):
        x1 = x1t[p]
        Vt = V_tm[p]
        y = y_rw[:, blk, :]
        y3 = y.rearrange("p (h n) -> p h n", h=8)
        self.op("dve", lambda e: e.tensor_reduce(hst[:, 0:8], y3, axis=AX.X, op=ALU.add), R=[y_rw], W=[hst])
        self.op("act", lambda e: e.activation(out=ysq[:], in_=y, func=AF.Square), R=[y_rw], W=[ysq])
        yield
        self.op("dve", lambda e: e.tensor_reduce(hst[:, 8:16], ysq[:].rearrange("p (h n) -> p h n", h=8), axis=AX.X, op=ALU.add),
                R=[ysq], W=[hst])
        self.op("dve", lambda e: e.tensor_scalar(hst[:, 16:24], hst[:, 0:8], 1.0 / 64, None, op0=ALU.mult), R=[hst], W=[hst])
        self.op("dve", lambda e: e.tensor_tensor(hst[:, 24:32], hst[:, 16:24], hst[:, 16:24], op=ALU.mult), R=[hst], W=[hst])
        self.op("dve", lambda e: e.scalar_tensor_tensor(hst[:, 32:40], hst[:, 8:16], 1.0 / 64, hst[:, 24:32],
                                                        op0=ALU.mult, op1=ALU.subtract), R=[hst], W=[hst])
        self.op("dve", lambda e: e.tensor_scalar(hst[:, 32:40], hst[:, 32:40], 64e-5, None, op0=ALU.add), R=[hst], W=[hst])
        yield
        self.op("pool", lambda e: e.tensor_tensor(hst[:, 40:48], hst[:, 32:40], self.mhalf[:, 0:8], op=ALU.pow),
                R=[hst, self.mhalf], W=[hst])
        self.op("pool", lambda e: e.tensor_tensor(t2[:].rearrange("p (h n) -> p h n", h=8),
                                                  Vt[:, blk, :].rearrange("p (h n) -> p h n", h=8),
                                                  bs_sb[p][:, blk * 8:(blk + 1) * 8].unsqueeze(2).to_broadcast([128, 8, 64]), op=ALU.mult),
                R=[Vt, bs_sb[p]], W=[t2])
        yield
        self.op("dve", lambda e: e.scalar_tensor_tensor(hst[:, 48:56], hst[:, 16:24], -1.0, hst[:, 40:48], op0=ALU.mult, op1=ALU.mult),
                R=[hst], W=[hst])
        yield
        for h in range(8):
            hsl = slice(h * 64, (h + 1) * 64)
            self.op("act", lambda e, h=h, hsl=hsl: e.activation(out=t1[:, hsl], in_=y_rw[:, blk, hsl], func=AF.Identity,
                                                                bias=hst[:, 48 + h:49 + h], scale=hst[:, 40 + h:41 + h]),
                    R=[y_rw, hst], W=[t1])
        yield
        self.op("dve", lambda e: e.tensor_tensor(t1[:], t1[:], lnw[:], op=ALU.mult), R=[t1, lnw], W=[t1])
        self.op("dve", lambda e: e.tensor_tensor(t1[:], t1[:], lnb[:], op=ALU.add), R=[t1, lnb], W=[t1])
        yield
        self.op("dve", lambda e: e.tensor_tensor(t1[:], t1[:], t2[:], op=ALU.add), R=[t1, t2], W=[t1])
        self.op("dve", lambda e: e.tensor_tensor(mrw[:], t1[:], g_sb[p][:, blk, :], op=ALU.mult), R=[t1, g_sb[p]], W=[mrw])
        yield
        pb = self.bankb[6]
        for k in range(4):
            self.op("pe", lambda e, k=k: e.transpose(pb[:, k * 128:(k + 1) * 128], mrw[:, k * 128:(k + 1) * 128], self.idb[:]),
                    R=[mrw, self.idb], W=[rb[6]])
        self.op("act", lambda e: e.activation(out=mixT[:].rearrange("p k t -> p (k t)"), in_=pb[:, 0:512], func=AF.Copy),
                R=[rb[6]], W=[mixT])
        yield
        for n in range(2):
            bank = 7 - n
            for k in range(4):
                self.op("pe", lambda e, n=n, k=k, bank=bank: e.matmul(B[bank][:, :], mixT[:, k, :], Wo[:, k, n * 512:(n + 1) * 512],
                                                                      start=(k == 0), stop=(k == 3)), R=[mixT, Wo], W=[rb[bank]])
            self.op("dve", lambda e, n=n, bank=bank: e.tensor_tensor(x1[:, blk, n * 512:(n + 1) * 512], x1[:, blk, n * 512:(n + 1) * 512],
                                                                     B[bank][:, :], op=ALU.add), R=[x1, rb[bank]], W=[x1])
            yield

    def chain_gens(*gens):
        for g in gens:
            if g is not None:
                yield from g

    tiles = [(seq, ti) for seq in range(self.n_seq) for ti in range(self.n_tok // TT)]

    def streamP_all():
        pending = None
        for i, (seq, ti) in enumerate(tiles):
            p = i % 2
            first = (i == 0)
            yield ("start", i)
            for _ in interleave(pending, chain_gens(head(seq, ti, p), prepA(0, p, first))):
                yield ("step", i)
            if i > 0:
                yield ("ready", i - 1)
            for j in range(3):
                for _ in interleave(neumann(j, p), prepA(j + 1, p, first)):
                    yield ("step", i)
            pending = neumann(3, p)
        for _ in pending:
            yield ("step", len(tiles) - 1)
        yield ("ready", len(tiles) - 1)

    def streamC(i):
        seq, ti = tiles[i]
        p = i % 2
        row0 = seq * self.n_tok + ti * TT
        if ti == 0:
            self.op("dve", lambda e: e.memset(S[:], 0.0), W=[S])
            self.op("dve", lambda e: e.memset(SBD[:], 0.0), W=[SBD])
        for c in range(NCH):
            yield from chain(c, p)
        if i == 0:
            self.dump("d_yrw", y_rw)
        for blk in range(NB):
            yield from finish(blk, p)
        self.dma(self.xm[row0:row0 + TT, :].rearrange("(b p) d -> p b d", p=128), x1t[p][:], R=[x1t[p]])

    Cact = None
    Cidx = -1
    for kind, i in streamP_all():
        if kind == "start":
            if Cact is not None and Cidx <= i - 2:
                drain(Cact)
                Cact = None
        elif kind == "ready":
            drain(Cact)
            Cact = streamC(i)
            Cidx = i
        if Cact is not None:
            try:
                next(Cact)
            except StopIteration:
                Cact = None
    drain(Cact)


Builder.phase1b = _phase1b
```

```python
import numpy as np
import concourse.bass as bass
import concourse.mybir as mybir
from concourse.bass_utils import run_bass_kernel_spmd

F32 = mybir.dt.float32
BF16 = mybir.dt.bfloat16
AF = mybir.ActivationFunctionType
ALU = mybir.AluOpType
AX = mybir.AxisListType

COMPUTE = ("pe", "act", "dve", "pool")


class Res:
    __slots__ = ("name", "last_w", "readers", "excl")

    def __init__(self, name, excl=False):
        self.name = name
        self.last_w = None
        self.readers = []
        self.excl = excl


class Op:
    __slots__ = ("eng", "fn", "deps", "signal", "semval", "dsem", "dtarget", "idx", "nm")

    def __init__(self, eng, fn, nm=""):
        self.eng = eng
        self.fn = fn
        self.deps = []
        self.signal = False
        self.semval = 0
        self.dsem = None
        self.dtarget = 0
        self.nm = nm


class Prog:
    def __init__(self, nc, n_dma_sems=8):
        self.nc = nc
        self.ops = []
        self.engs = {"pe": nc.tensor, "act": nc.scalar, "dve": nc.vector, "pool": nc.gpsimd,
                     "sp": nc.sync}
        self.n_dma_sems = n_dma_sems
        self._stack = []
        self.after = None
        self.last_on = {}
        self.dma_ops = []

    def keep(self, cm):
        h = cm.__enter__()
        self._stack.append(cm)
        return h

    def sb(self, name, shape, dt):
        return self.keep(self.nc.sbuf_tensor(name, list(shape), dt))

    def op(self, eng, fn, reads=(), writes=(), nm=""):
        o = Op(eng, fn, nm)
        o.idx = len(self.ops)
        deps = {}
        is_dma = eng not in COMPUTE

        def add(d, raw):
            if d is None or d is o:
                return
            if d.eng == eng and not is_dma:
                if not raw or eng == "pe":
                    return
            deps[id(d)] = d

        for r in reads:
            if r.excl:
                add(r.last_w, True)
                for x in r.readers:
                    add(x, True)
                r.last_w = o
                r.readers = []
            else:
                add(r.last_w, True)
                r.readers.append(o)
        for w in writes:
            add(w.last_w, w.excl)
            for x in w.readers:
                add(x, w.excl)
            w.last_w = o
            w.readers = []
        if self.after is not None:
            deps[id(self.after)] = self.after
        o.deps = list(deps.values())
        self.ops.append(o)
        self.last_on[eng] = o
        if is_dma:
            self.dma_ops.append(o)
        return o

    def mark(self):
        return len(self._stack)

    def barrier(self, fn):
        prev = [v for k, v in self.last_on.items() if k in COMPUTE] + self.dma_ops[-self.n_dma_sems:]
        o = self.op("sp", fn)
        have = {id(d) for d in o.deps}
        for d in prev:
            if id(d) not in have and d is not o:
                o.deps.append(d)
        self.after = o
        return o

    def release_to(self, mark):
        while len(self._stack) > mark:
            cm = self._stack.pop()
            cm.__exit__(None, None, None)

    def emit(self):
        nc = self.nc
        for o in self.ops:
            for d in o.deps:
                d.signal = True
        sems = {e: self.keep(nc.semaphore("s_" + e)) for e in COMPUTE}
        dsems = [self.keep(nc.semaphore("s_dma%d" % i)) for i in range(self.n_dma_sems)]
        cnt = {e: 0 for e in COMPUTE}
        ndma = 0
        dma_hist = []
        for o in self.ops:
            if o.eng in COMPUTE:
                if o.signal:
                    cnt[o.eng] += 1
                    o.semval = cnt[o.eng]
            else:
                k = ndma % self.n_dma_sems
                o.dsem = dsems[k]
                o.dtarget = 16 * (ndma // self.n_dma_sems + 1)
                dma_hist.append(o)
                ndma += 1
        seen = {}
        ndma = 0
        for o in self.ops:
            e = self.engs[o.eng]
            waits = []
            for d in o.deps:
                if d.eng in COMPUTE:
                    key = (o.eng, d.eng)
                    if seen.get(key, 0) >= d.semval:
                        continue
                    seen[key] = d.semval
                    waits.append((sems[d.eng], d.semval))
                else:
                    key = (o.eng, id(d.dsem))
                    if seen.get(key, 0) >= d.dtarget:
                        continue
                    seen[key] = d.dtarget
                    waits.append((d.dsem, d.dtarget))
            if o.eng not in COMPUTE:
                if ndma >= self.n_dma_sems:
                    p = dma_hist[ndma - self.n_dma_sems]
                    key = (o.eng, id(p.dsem))
                    if seen.get(key, 0) < p.dtarget:
                        seen[key] = p.dtarget
                        waits.append((p.dsem, p.dtarget))
                ndma += 1
            for s, v in waits:
                e.wait_ge(s, v)
            ins = o.fn(e)
            if o.eng in COMPUTE:
                if o.signal:
                    ins.then_inc(sems[o.eng], 1)
            else:
                ins.then_inc(o.dsem, 16)
        tail = self.engs["sp"]
        start = max(0, len(dma_hist) - self.n_dma_sems)
        for p in dma_hist[start:]:
            tail.wait_ge(p.dsem, p.dtarget)


D = 1024
SEQ = 2048
NCORE = 8
SEQ_PER_CORE = 4
DFF = 2816
NFF = DFF // 128
DPLE = 256
RW = 512
EPS = 1e-6
C_DEC = float(np.exp(-0.5))
DH = 128
LOG_G = [float(np.log(1.0 - 2.0 ** (-5.0 - h))) for h in range(4)]

PV = {}
_o = 0
for _n, _w in (("nw_mix", 8), ("ret_nw", 4), ("mu", 14), ("w0", 4), ("a0", 4), ("k_k", 4),
               ("k_a", 4), ("r_k", 4), ("nw_ffn", 8), ("nw_ple", 8)):
    PV[_n] = (_o, _w)
    _o += _w
NPV = _o

CS = {}
_o = 0
for _n, _w in (("ident", 128),
               ("maskT", 128), ("ghat", 4), ("epsr", 4), ("gC", 512), ("gCd", 512),
               ("mask4", 512), ("scanmask", 256), ("bones", 128), ("sel2", 2), ("bd4", 512)):
    CS[_n] = (_o, _w)
    _o += _w
NCS = _o
CA0, CA1 = CS["maskT"][0], CS["gCd"][0] + CS["gCd"][1]
CB0, CB1 = CS["mask4"][0], CS["bd4"][0] + CS["bd4"][1]


def host_consts():
    c = np.zeros((128, NCS), np.float32)

    def put(name, arr):
        o, w = CS[name]
        c[:, o:o + w] = arr.reshape(128, w)

    p = np.arange(128)
    put("ident", np.eye(128, dtype=np.float32))
    put("maskT", (p[:, None] <= p[None, :]).astype(np.float32) * (DH ** -0.5))
    lg = np.array(LOG_G, np.float64)
    put("ghat", np.exp(-lg[None, :] * (p[:, None] + 1.0)).astype(np.float32))
    put("epsr", (1e-5 * np.exp(-2.0 * lg[None, :] * (p[:, None] + 1.0))).astype(np.float32))
    gC = np.exp(lg * 128.0)
    put("gC", np.broadcast_to(np.repeat(gC, 128)[None, :], (128, 512)).astype(np.float32))
    put("gCd", np.broadcast_to(np.repeat(gC * DH ** -0.5, 128)[None, :], (128, 512)).astype(np.float32))
    same = (p[:, None] // 64) == (p[None, :] // 64)
    su = (same & (p[:, None] < p[None, :])).astype(np.float32)
    iu = (same & (p[:, None] <= p[None, :])).astype(np.float32)
    put("mask4", np.concatenate([su, iu, su, iu], axis=1))
    sm = np.ones((128, 256), np.float32)
    sm[:, 0::64] = 0.0
    put("scanmask", sm)
    put("bones", same.astype(np.float32))
    put("sel2", (p[:, None] // 64 == np.arange(2)[None, :]).astype(np.float32))
    put("bd4", np.tile(same.astype(np.float32), (1, 4)))
    pos = (np.arange(16)[None, :] * 128 + p[:, None]).astype(np.float32)
    angle = (1.0 / (10000.0 ** np.linspace(0.0, 1.0, DH // 2, dtype=np.float32))).astype(np.float32)
    angle = np.repeat(angle, 2)
    phase = (pos[:, :, None] * angle[None, None, :]).astype(np.float32)
    cos = np.cos(phase).astype(np.float32)
    sin = np.sin(phase).astype(np.float32)
    sgn = np.where(np.arange(DH) % 2 == 0, -1.0, 1.0).astype(np.float32)
    rot = np.stack([cos, sin * sgn[None, None, :]], axis=1).astype(np.float32)
    return c, np.ascontiguousarray(rot.reshape(128, 2 * 16 * 128))


class Tile:
    def __init__(self, P, name, shape, dt):
        self.h = P.sb(name, shape, dt)
        self.r = Res(name)

    def __getitem__(self, k):
        return self.h[k]


class Builder:
    def __init__(self, n_seq=SEQ_PER_CORE, n_tok=SEQ, phases=("1a", "1b", "2"), dbg=None):
        self.n_seq = n_seq
        self.n_tok = n_tok
        self.phases = phases
        self.dbg = dbg or ()
        self.dbg_outs = {}
        nc = bass.Bass("TRN2", target_bir_lowering=False)
        self.nc = nc
        self.P = Prog(nc, n_dma_sems=16)
        NT = n_seq * n_tok
        self.NT = NT

        def din(name, shape):
            return nc.dram_tensor(name, list(shape), F32, kind="ExternalInput").ap()

        self.x = din("x", [NT, D])
        self.p = din("p", [NT, DPLE])
        self.w_in = din("w_in", [D, 3840])
        self.w_o = din("w_o", [D, D])
        self.w_gate = din("w_gate", [D, DFF])
        self.w_up = din("w_up", [D, DFF])
        self.w_down = din("w_down", [DFF, D])
        self.w_pg = din("w_ple_gate", [D, D])
        self.w_pu = din("w_ple_up", [DPLE, D])
        self.w2 = din("rw_w2", [64, RW])
        self.a2 = din("rw_a2", [64, RW])
        self.g2 = din("rw_g2", [128, RW])
        self.pv_d = din("pv", [128, NPV])
        self.pb_d = din("pb", [128, 2048])
        self.cst_d = din("cst", [128, NCS])
        self.rot_d = din("rot", [128, 4096])
        self.out = nc.dram_tensor("out", [NT, D], F32, kind="ExternalOutput").ap()
        self.x1 = nc.dram_tensor("x1_scr", [NT, D], F32, kind="Internal").ap()
        self.xm = nc.dram_tensor("xm_scr", [NT, D], F32, kind="Internal").ap()
        P = self.P
        self.banks = [P.keep(nc.psum_tensor("ps%d" % i, [128, 512], F32)) for i in range(8)]
        self.rb = [Res("bank%d" % i, excl=True) for i in range(8)]
        self.bankb = [b[:].bitcast(BF16) for b in self.banks]
        self.uid = 0

    def T(self, name, shape, dt):
        self.uid += 1
        return Tile(self.P, "%s_%d" % (name, self.uid), shape, dt)

    def op(self, eng, fn, R=(), W=()):
        rs = [t.r if isinstance(t, Tile) else t for t in R]
        ws = [t.r if isinstance(t, Tile) else t for t in W]
        return self.P.op(eng, fn, reads=rs, writes=ws)

    def dma(self, out, in_, R=(), W=()):
        return self.op("sp", lambda e: e.dma_start(out=out, in_=in_), R, W)

    def dump(self, name, tile, ap=None):
        if name not in self.dbg:
            return
        ap = tile[:] if ap is None else ap
        shape = list(ap.shape)
        if ap.dtype != F32:
            tmp = self.T("dbg", shape, F32)
            idx = tuple(slice(None) for _ in shape)
            self.op("act", lambda e: e.activation(out=tmp[idx], in_=ap, func=AF.Copy), R=[tile], W=[tmp])
            src, srct = tmp[idx], tmp
        else:
            src, srct = ap, tile
        o = self.nc.dram_tensor(name, shape, F32, kind="ExternalOutput").ap()
        self.dbg_outs[name] = shape
        self.dma(o, src, R=[srct])

    def phase_barrier(self, mark):
        self.P.barrier(lambda e: e.dma_start(out=self.bdummy[0:1, 0:16], in_=self.cst_d[0:1, 0:16]))
        self.P.release_to(mark)

    def setup_common(self):
        self.pv = self.T("pv", [128, NPV], F32)
        self.dma(self.pv[:], self.pv_d, W=[self.pv])
        self.stage_i = 0
        self.bdummy = self.T("bdummy", [1, 16], F32)
        self.idb = self.T("idb", [128, 128], BF16)
        idf = self.T("idf", [128, 128], F32)
        self.dma(idf[:], self.cst_d[:, 0:128], W=[idf])
        self.op("dve", lambda e: e.tensor_copy(self.idb[:], idf[:]), R=[idf], W=[self.idb])
        self.mhalf = self.T("mhalf", [128, 64], F32)
        self.op("pool", lambda e: e.memset(self.mhalf[:], -0.5), W=[self.mhalf])
        self.dv = self.T("dv", [128, 32], F32)

        def derive(dst0, n, name, s1, s2):
            o = PV[name][0]
            if s2 is None:
                self.op("dve", lambda e: e.tensor_scalar(self.dv[:, dst0:dst0 + n], self.pv[:, o:o + n], s1, None, op0=ALU.mult),
                        R=[self.pv], W=[self.dv])
            else:
                self.op("dve", lambda e: e.tensor_scalar(self.dv[:, dst0:dst0 + n], self.pv[:, o:o + n], s1, s2,
                                                         op0=ALU.mult, op1=ALU.add), R=[self.pv], W=[self.dv])

        derive(0, 14, "mu", -1.0, 1.0)
        derive(14, 4, "w0", 0.5, None)
        derive(18, 4, "a0", 0.5, None)
        derive(22, 4, "k_k", -1.0, None)
        derive(26, 4, "k_a", -1.0, 1.0)

    def alloc_stage(self):
        self.stage_mark = self.P.mark()
        self.stage = [self.T("stage", [128, 1024], F32) for _ in range(6)]

    def free_stage(self):
        self.phase_barrier(self.stage_mark)

    def pvc(self, name, j=None):
        o, w = PV[name]
        if j is None:
            return self.pv[:, o:o + w]
        return self.pv[:, o + j:o + j + 1]

    def load_w(self, dst_tile, dst_ap, src_ap, ncols, scale_ap=None, scale_imm=None, rows=128, prow=0):
        st = self.stage[self.stage_i % 6]
        eng = "dve" if self.stage_i % 2 == 0 else "act"
        self.stage_i += 1
        sl = st[prow:prow + rows, 0:ncols]
        self.dma(sl, src_ap, W=[st])
        extra = [self.pv] if scale_ap is not None else []
        if scale_ap is not None and scale_imm is not None:
            self.op("dve", lambda e: e.tensor_scalar(dst_ap, sl, scale_ap, scale_imm, op0=ALU.mult, op1=ALU.mult),
                    R=[st] + extra, W=[dst_tile])
        elif eng == "dve":
            if scale_ap is not None:
                self.op("dve", lambda e: e.tensor_scalar(dst_ap, sl, scale_ap, None, op0=ALU.mult),
                        R=[st] + extra, W=[dst_tile])
            else:
                self.op("dve", lambda e: e.tensor_copy(dst_ap, sl), R=[st], W=[dst_tile])
        else:
            if scale_ap is not None:
                self.op("act", lambda e: e.activation(out=dst_ap, in_=sl, func=AF.Copy, scale=scale_ap),
                        R=[st] + extra, W=[dst_tile])
            else:
                self.op("act", lambda e: e.activation(out=dst_ap, in_=sl, func=AF.Copy), R=[st], W=[dst_tile])

    def load_w_cols(self, dst_tile, dst_fn, src, r0, c0, ncols, scale_ap=None, scale_imm=None):
        for a in range(0, ncols, 1024):
            n = min(1024, ncols - a)
            self.load_w(dst_tile, dst_fn(a, a + n), src[r0:r0 + 128, c0 + a:c0 + a + n], n, scale_ap, scale_imm)

    def norm_only(self, x_tm, nblk, xs, hnT, ss, vv, rstd, junk, bank):
        for b in range(nblk):
            self.op("act", lambda e, b=b: e.activation(out=junk[:], in_=x_tm[:, b, :], func=AF.Square,
                                                        accum_out=ss[:, b:b + 1]),
                    R=[x_tm], W=[junk, ss])
        self.op("dve", lambda e: e.tensor_scalar(vv[:, 0:nblk], ss[:, 0:nblk], 1.0 / D, EPS, op0=ALU.mult, op1=ALU.add),
                R=[ss], W=[vv])
        self.op("pool", lambda e: e.tensor_tensor(rstd[:, 0:nblk], vv[:, 0:nblk], self.mhalf[:, 0:nblk], op=ALU.pow),
                R=[vv, self.mhalf], W=[rstd])
        for b in range(nblk):
            self.op("act", lambda e, b=b: e.activation(out=xs[:, b, :], in_=x_tm[:, b, :], func=AF.Copy,
                                                        scale=rstd[:, b:b + 1]),
                    R=[x_tm, rstd], W=[xs])
        pb = self.bankb[bank]
        for b in range(nblk):
            for k in range(8):
                self.op("pe", lambda e, b=b, k=k: e.transpose(pb[:, k * 128:(k + 1) * 128],
                                                               xs[:, b, k * 128:(k + 1) * 128], self.idb[:]),
                        R=[xs, self.idb], W=[self.rb[bank]])
            self.op("dve", lambda e, b=b: e.tensor_copy(hnT[:, :, b * 128:(b + 1) * 128],
                                                         pb[:, 0:1024].rearrange("p (k t) -> p k t", k=8)),
                    R=[self.rb[bank]], W=[hnT])

    def phase2(self, src, src_res):
        TT, NB = 128, 1
        Wgu = self.T("Wgu", [128, 8, 2, DFF], BF16)
        Wd = self.T("Wd", [128, NFF, D], BF16)
        Wpg = self.T("Wpg", [128, 8, D], BF16)
        Wpu = self.T("Wpu", [128, 2, D], BF16)
        fnw = self.T("fnw", [128, D], F32)
        self.dma(fnw[:], self.pb_d[:, 1024:2048], W=[fnw])
        self.alloc_stage()
        for k in range(8):
            for gi, wsrc in enumerate((self.w_gate, self.w_up)):
                self.load_w_cols(Wgu, lambda lo, hi, k=k, gi=gi: Wgu[:, k, gi, lo:hi], wsrc, k * 128, 0, DFF,
                                 scale_ap=self.pvc("nw_ffn", k))
        for f in range(NFF):
            self.load_w(Wd, Wd[:, f, :], self.w_down[f * 128:(f + 1) * 128, :], D)
        for k in range(8):
            self.load_w(Wpg, Wpg[:, k, :], self.w_pg[k * 128:(k + 1) * 128, :], D, scale_ap=self.pvc("nw_ple", k))
        for k in range(2):
            self.load_w(Wpu, Wpu[:, k, :], self.w_pu[k * 128:(k + 1) * 128, :], D)
        self.free_stage()

        x_tm = [self.T("x2tm", [128, NB, D], F32) for _ in range(3)]
        p_tm = [self.T("p2tm", [128, NB, DPLE], F32) for _ in range(3)]
        xsA = self.T("x2sA", [128, NB, D], BF16)
        xsB = self.T("x2sB", [128, NB, D], BF16)
        hTA2 = [self.T("hfTA", [128, 8, TT], BF16) for _ in range(2)]
        hTB = self.T("hfTB", [128, 8, TT], BF16)
        junkA = self.T("junk2A", [128, D], BF16)
        junkB = self.T("junk2B", [128, D], BF16)
        stA = [self.T("stA", [128, 4], F32) for _ in range(3)]
        stB = [self.T("stB", [128, 4], F32) for _ in range(3)]
        stC = [self.T("stC", [128, 4], F32) for _ in range(3)]
        sil = [self.T("sil", [128, TT], F32) for _ in range(4)]
        hh = self.T("hh", [128, NFF, TT], BF16)
        pbf = self.T("pbf", [128, NB, DPLE], BF16)
        pT = self.T("pT", [128, 2, TT], BF16)
        th = self.T("thp", [128, 512], F32)
        tt = self.T("ttp", [128, 512], F32)
        ntile = self.NT // TT
        B = self.banks
        rb = self.rb

        def headA1(ti):
            row0 = ti * TT
            sl = ti % 3
            xt = x_tm[sl]
            rres = [src_res[row0 // 128]] if src_res is not None else []
            self.dma(xt[:], src[row0:row0 + TT, :].rearrange("(b p) d -> p b d", p=128), R=rres, W=[xt])
            self.dma(p_tm[sl][:], self.p[row0:row0 + TT, :].rearrange("(b p) d -> p b d", p=128), W=[p_tm[sl]])
            ss, vv, rstd = stA
            self.op("act", lambda e: e.activation(out=junkA[:], in_=xt[:, 0, :], func=AF.Square, accum_out=ss[:, 0:1]),
                    R=[xt], W=[junkA, ss])
            self.op("dve", lambda e: e.tensor_scalar(vv[:, 0:1], ss[:, 0:1], 1.0 / D, EPS, op0=ALU.mult, op1=ALU.add),
                    R=[ss], W=[vv])
            self.op("pool", lambda e: e.tensor_tensor(rstd[:, 0:1], vv[:, 0:1], self.mhalf[:, 0:1], op=ALU.pow),
                    R=[vv, self.mhalf], W=[rstd])
            self.op("act", lambda e: e.activation(out=xsA[:, 0, :], in_=xt[:, 0, :], func=AF.Copy, scale=rstd[:, 0:1]),
                    R=[xt, rstd], W=[xsA])

        def headA2(ti):
            hT_ = hTA2[ti % 2]
            pb = self.bankb[0]
            for k in range(8):
                self.op("pe", lambda e, k=k: e.transpose(pb[:, k * 128:(k + 1) * 128], xsA[:, 0, k * 128:(k + 1) * 128], self.idb[:]),
                        R=[xsA, self.idb], W=[rb[0]])
            self.op("dve", lambda e: e.tensor_copy(hT_[:].rearrange("p k t -> p (k t)"), pb[:, 0:1024]), R=[rb[0]], W=[hT_])

        def headA(ti):
            headA1(ti)
            headA2(ti)

        def stageA(ti):
            xt = x_tm[ti % 3]
            hTA = hTA2[ti % 2]
            for f in range(NFF):
                bank = 1 + (f % 4)
                for gi in range(2):
                    for k in range(8):
                        self.op("pe", lambda e, f=f, gi=gi, k=k, bank=bank: e.matmul(
                            B[bank][:, gi * TT:(gi + 1) * TT], Wgu[:, k, gi, f * 128:(f + 1) * 128],
                            hTA[:, k, :], start=(k == 0), stop=(k == 7)), R=[Wgu, hTA], W=[rb[bank]])
                s = sil[f % 4]
                self.op("act", lambda e, s=s, bank=bank: e.activation(out=s[:], in_=B[bank][:, 0:TT], func=AF.Silu),
                        R=[rb[bank]], W=[s])
                self.op("dve", lambda e, s=s, f=f, bank=bank: e.tensor_tensor(hh[:, f, :], s[:], B[bank][:, TT:2 * TT], op=ALU.mult),
                        R=[s, rb[bank]], W=[hh])
                if f == 5 and ti + 1 < ntile:
                    headA1(ti + 1)
                if f == 14 and ti + 1 < ntile:
                    headA2(ti + 1)
                yield
            for n in range(2):
                bank = 3 + n
                for f in range(NFF):
                    self.op("pe", lambda e, n=n, f=f, bank=bank: e.matmul(
                        B[bank][:, :], hh[:, f, :], Wd[:, f, n * 512:(n + 1) * 512],
                        start=(f == 0), stop=(f == NFF - 1)), R=[hh, Wd], W=[rb[bank]])
                self.op("dve", lambda e, n=n, bank=bank: e.tensor_tensor(
                    xt[:, 0, n * 512:(n + 1) * 512], xt[:, 0, n * 512:(n + 1) * 512], B[bank][:, :], op=ALU.add),
                    R=[xt, rb[bank]], W=[xt])
                yield

        def stageB(ti):
            row0 = ti * TT
            sl = ti % 3
            xt = x_tm[sl]
            ssb, vvb, rsb = stB
            self.op("act", lambda e: e.activation(out=junkB[:], in_=xt[:, 0, :], func=AF.Square, accum_out=ssb[:, 0:1]),
                    R=[xt], W=[junkB, ssb])
            self.op("dve", lambda e: e.tensor_scalar(vvb[:, 0:1], ssb[:, 0:1], 1.0 / D, EPS, op0=ALU.mult, op1=ALU.add),
                    R=[ssb], W=[vvb])
            self.op("pool", lambda e: e.tensor_tensor(rsb[:, 0:1], vvb[:, 0:1], self.mhalf[:, 0:1], op=ALU.pow),
                    R=[vvb, self.mhalf], W=[rsb])
            self.op("act", lambda e: e.activation(out=xsB[:, 0, :], in_=xt[:, 0, :], func=AF.Copy, scale=rsb[:, 0:1]),
                    R=[xt, rsb], W=[xsB])
            yield
            pbn = self.bankb[5]
            for k in range(8):
                self.op("pe", lambda e, k=k: e.transpose(pbn[:, k * 128:(k + 1) * 128], xsB[:, 0, k * 128:(k + 1) * 128], self.idb[:]),
                        R=[xsB, self.idb], W=[rb[5]])
            self.op("dve", lambda e: e.tensor_copy(hTB[:].rearrange("p k t -> p (k t)"), pbn[:, 0:1024]), R=[rb[5]], W=[hTB])
            yield
            self.op("act", lambda e: e.activation(out=pbf[:], in_=p_tm[sl][:], func=AF.Copy), R=[p_tm[sl]], W=[pbf])
            pb5 = self.bankb[5]
            for k in range(2):
                self.op("pe", lambda e, k=k: e.transpose(pb5[:, k * 128:(k + 1) * 128], pbf[:, 0, k * 128:(k + 1) * 128], self.idb[:]),
                        R=[pbf, self.idb], W=[rb[5]])
            self.op("act", lambda e: e.activation(out=pT[:].rearrange("p k t -> p (k t)"), in_=pb5[:, 0:256], func=AF.Copy),
                    R=[rb[5]], W=[pT])
            yield
            for n in range(2):
                for k in range(8):
                    self.op("pe", lambda e, n=n, k=k: e.matmul(B[6][:, :], hTB[:, k, :], Wpg[:, k, n * 512:(n + 1) * 512],
                                                               start=(k == 0), stop=(k == 7)), R=[hTB, Wpg], W=[rb[6]])
                for k in range(2):
                    self.op("pe", lambda e, n=n, k=k: e.matmul(B[7][:, :], pT[:, k, :], Wpu[:, k, n * 512:(n + 1) * 512],
                                                               start=(k == 0), stop=(k == 1)), R=[pT, Wpu], W=[rb[7]])
                yield
                self.op("act", lambda e: e.activation(out=th[:], in_=B[6][:, :], func=AF.Tanh, scale=0.5), R=[rb[6]], W=[th])
                self.op("dve", lambda e: e.scalar_tensor_tensor(tt[:], th[:], 1.0, B[7][:, :], op0=ALU.add, op1=ALU.mult),
                        R=[th, rb[7]], W=[tt])
                self.op("dve", lambda e, n=n: e.scalar_tensor_tensor(
                    xt[:, 0, n * 512:(n + 1) * 512], tt[:], 0.5, xt[:, 0, n * 512:(n + 1) * 512],
                    op0=ALU.mult, op1=ALU.add), R=[tt, xt], W=[xt])
                yield
            ss, vv, rstd = stC
            self.op("act", lambda e: e.activation(out=junkB[:], in_=xt[:, 0, :], func=AF.Square, accum_out=ss[:, 0:1]),
                    R=[xt], W=[junkB, ss])
            self.op("dve", lambda e: e.tensor_scalar(vv[:, 0:NB], ss[:, 0:NB], 1.0 / D, EPS, op0=ALU.mult, op1=ALU.add),
                    R=[ss], W=[vv])
            self.op("pool", lambda e: e.tensor_tensor(rstd[:, 0:NB], vv[:, 0:NB], self.mhalf[:, 0:NB], op=ALU.pow),
                    R=[vv, self.mhalf], W=[rstd])
            yield
            self.op("dve", lambda e: e.scalar_tensor_tensor(xt[:, 0, :], xt[:, 0, :], rstd[:, 0:1], fnw[:],
                                                            op0=ALU.mult, op1=ALU.mult), R=[xt, rstd, fnw], W=[xt])
            self.dma(self.out[row0:row0 + TT, :].rearrange("(b p) d -> p b d", p=128), xt[:], R=[xt])

        headA(0)
        prev = None
        for ti in range(ntile):
            merge_ratio(stageA(ti), prev, 3)
            prev = stageB(ti)
        drain(prev)

    def build(self):
        self.setup_common()
        m0 = self.P.mark()
        src, src_res = self.x, None
        if "1a" in self.phases:
            self.phase1a()
            self.phase_barrier(m0)
            src, src_res = self.x1, None
        if "1b" in self.phases:
            self.phase1b(src)
            self.phase_barrier(m0)
            src, src_res = self.xm, None
        if "2" in self.phases:
            self.phase2(src, src_res)
        else:
            for r0 in range(0, self.NT, 256):
                self.dma(self.out[r0:r0 + 256, :], src[r0:r0 + 256, :])
        self.P.emit()
        return self.nc


def host_layout(inputs, n_core=NCORE, n_seq=SEQ_PER_CORE, n_tok=SEQ):
    f = lambda a: np.ascontiguousarray(np.asarray(a, dtype=np.float32))
    x = f(inputs["x"])
    p = f(inputs["p"])[0]
    cst, rot = host_consts()
    pv = np.zeros((128, NPV), np.float32)

    def putv(name, vec, ncol):
        o, w = PV[name]
        pv[:, o:o + w] = np.asarray(vec, np.float32).reshape(ncol, 128).T

    putv("nw_mix", inputs["norm_mix_w"][0], 8)
    putv("ret_nw", inputs["ret_norm_w"][0], 4)
    putv("mu", inputs["rw_mu"][0], 14)
    putv("w0", inputs["rw_w0"][0], 4)
    putv("a0", inputs["rw_a0"][0], 4)
    putv("k_k", inputs["rw_k_k"][0], 4)
    putv("k_a", inputs["rw_k_a"][0], 4)
    putv("r_k", np.asarray(inputs["rw_r_k"][0]).reshape(-1), 4)
    putv("nw_ffn", inputs["norm_ffn_w"][0], 8)
    putv("nw_ple", inputs["norm_ple_w"][0], 8)
    pb = np.concatenate([np.asarray(inputs["rw_ln_w"][0], np.float32), np.asarray(inputs["rw_ln_b"][0], np.float32),
                         np.asarray(inputs["final_norm_w"], np.float32)])
    pb = np.ascontiguousarray(np.broadcast_to(pb[None, :], (128, 2048)))
    shared = {
        "w_in": f(inputs["w_in"][0]), "w_o": f(inputs["w_o"][0]), "w_gate": f(inputs["w_gate"][0]),
        "w_up": f(inputs["w_up"][0]), "w_down": f(inputs["w_down"][0]), "w_ple_gate": f(inputs["w_ple_gate"][0]),
        "w_ple_up": f(inputs["w_ple_up"][0]), "rw_w2": f(inputs["rw_w2"][0]), "rw_a2": f(inputs["rw_a2"][0]),
        "rw_g2": f(inputs["rw_g2"][0]), "pv": pv, "pb": pb, "cst": cst, "rot": rot,
    }
    maps = []
    for c in range(n_core):
        m = dict(shared)
        xs = x[c * n_seq:(c + 1) * n_seq, :n_tok, :].reshape(n_seq * n_tok, D)
        ps = p[c * n_seq:(c + 1) * n_seq, :n_tok, :].reshape(n_seq * n_tok, DPLE)
        m["x"] = np.ascontiguousarray(xs)
        m["p"] = np.ascontiguousarray(ps)
        maps.append(m)
    return maps


def kernel(**inputs):
    b = Builder()
    nc = b.build()
    maps = host_layout(inputs)
    res = run_bass_kernel_spmd(nc, maps, core_ids=list(range(NCORE)))
    outs = [np.asarray(r["out"], dtype=np.float32).reshape(SEQ_PER_CORE, SEQ, D) for r in res.results]
    return np.concatenate(outs, axis=0)


def merge(*gens):
    gens = [g for g in gens if g is not None]
    while gens:
        for g in list(gens):
            try:
                next(g)
            except StopIteration:
                gens.remove(g)


def merge_ratio(long_gen, short_gen, r):
    la, sa = long_gen is not None, short_gen is not None
    while la or sa:
        for _ in range(r):
            if la:
                try:
                    next(long_gen)
                except StopIteration:
                    la = False
        if sa:
            try:
                next(short_gen)
            except StopIteration:
                sa = False


def drain(g):
    if g is not None:
        for _ in g:
            pass


def _phase1a(self):
    TT, NB = 512, 4
    Wa = self.T("Wa", [128, 8, 2048], BF16)
    Wo = self.T("WoT", [128, 4, D], BF16)
    rot = self.T("rot", [128, 2, 16, 128], F32)
    ca = self.T("ca", [128, CA1 - CA0], F32)
    self.dma(rot[:].rearrange("p a b c -> p (a b c)"), self.rot_d, W=[rot])
    self.dma(ca[:], self.cst_d[:, CA0:CA1], W=[ca])

    def cc(name):
        o, w = CS[name]
        return ca[:, o - CA0:o - CA0 + w]

    self.alloc_stage()
    for k in range(8):
        self.load_w_cols(Wa, lambda lo, hi, k=k: Wa[:, k, lo:hi], self.w_in, k * 128, 0, 2048,
                         scale_ap=self.pvc("nw_mix", k))
    for k in range(4):
        self.load_w(Wo, Wo[:, k, :], self.w_o[k * 128:(k + 1) * 128, :], D,
                    scale_ap=self.pvc("ret_nw", k), scale_imm=0.5)
    maskT = self.T("maskTb", [128, 128], BF16)
    self.op("dve", lambda e: e.tensor_copy(maskT[:], cc("maskT")), R=[ca], W=[maskT])
    gCdb = self.T("gCdb", [128, 512], BF16)
    self.op("dve", lambda e: e.tensor_copy(gCdb[:], cc("gCd")), R=[ca], W=[gCdb])

    x_tm = [self.T("xa", [128, NB, D], F32) for _ in range(2)]
    xs = self.T("xsa", [128, NB, D], BF16)
    hnT = [self.T("hnTa", [128, 8, TT], BF16) for _ in range(2)]
    mixT = self.T("mixTa", [128, 4, 128], BF16)
    junk = self.T("junka", [128, D], BF16)
    ss = self.T("ssa", [128, 4], F32)
    vv = self.T("vva", [128, 4], F32)
    rstd = self.T("rstda", [128, 4], F32)

    def two(name, shape, dt):
        return [self.T(name, shape, dt) for _ in range(2)]

    tA = two("tA", [128, 1024], F32)
    tB = two("tB", [128, 1024], F32)
    qrot = two("qrot", [128, 512], BF16)
    ktmp = two("ktmp", [128, 512], F32)
    khat = two("khat", [128, 512], BF16)
    qkT = two("qkT", [128, 8, 128], BF16)
    v_tm = two("v_tm", [128, 512], BF16)
    vs = two("vs", [128, 512], BF16)
    thg = two("thg", [128, 512], F32)
    sg = two("sg", [128, 512], F32)
    sT = two("sT", [128, 512], BF16)
    ysb = self.T("ysb", [128, 512], F32)
    ysq = self.T("ysq", [128, 512], F32)
    yc = self.T("yc", [128, 512], F32)
    mret = self.T("mret", [128, 512], BF16)
    R = self.T("Rst", [128, 512], F32)
    Rb = self.T("Rbf", [128, 512], BF16)
    st = self.T("hst", [128, 24], F32)
    B = self.banks
    rb = self.rb

    def stage1(seq, ti, b, xt, hn, par):
        tk = slice(b * 128, (b + 1) * 128)
        posblk = ti * 4 + b
        tA_, tB_, qrot_, ktmp_, khat_, qkT_ = tA[par], tB[par], qrot[par], ktmp[par], khat[par], qkT[par]
        v_, vs_, thg_, sg_, sT_ = v_tm[par], vs[par], thg[par], sg[par], sT[par]
        for c in range(4):
            for k in range(8):
                self.op("pe", lambda e, c=c, k=k: e.matmul(B[c][:, :], hn[:, k, tk], Wa[:, k, c * 512:(c + 1) * 512],
                                                           start=(k == 0), stop=(k == 7)), R=[hn, Wa], W=[rb[c]])
            yield
        cosb = rot[:, 0, posblk, :].unsqueeze(1).to_broadcast([128, 4, 128])
        sinv = rot[:, 1, posblk, :].rearrange("p (i two) -> p i two", two=2)
        for c in range(2):
            pv4 = B[c][:, :].rearrange("p (h e) -> p h e", h=4)
            self.op("dve", lambda e, c=c, pv4=pv4: e.tensor_tensor(
                tA_[:, c * 512:(c + 1) * 512].rearrange("p (h e) -> p h e", h=4), pv4, cosb, op=ALU.mult),
                R=[rb[c], rot], W=[tA_])
            pe2 = B[c][:, :].rearrange("p (h i two) -> p h i two", h=4, two=2)
            tb2 = tB_[:, c * 512:(c + 1) * 512].rearrange("p (h i two) -> p h i two", h=4, two=2)
            for par2 in range(2):
                self.op("dve", lambda e, par2=par2, pe2=pe2, tb2=tb2: e.tensor_tensor(
                    tb2[:, :, :, par2], pe2[:, :, :, 1 - par2],
                    sinv[:, :, par2].unsqueeze(1).to_broadcast([128, 4, 64]), op=ALU.mult),
                    R=[rb[c], rot], W=[tB_])
            yield
        self.op("act", lambda e: e.activation(out=v_[:], in_=B[2][:, :], func=AF.Copy), R=[rb[2]], W=[v_])
        self.op("act", lambda e: e.activation(out=thg_[:], in_=B[3][:, :], func=AF.Tanh, scale=0.5), R=[rb[3]], W=[thg_])
        self.op("dve", lambda e: e.scalar_tensor_tensor(sg_[:], thg_[:], 1.0, B[3][:, :], op0=ALU.add, op1=ALU.mult),
                R=[thg_, rb[3]], W=[sg_])
        self.op("pool", lambda e: e.tensor_tensor(vs_[:], v_[:], gCdb[:], op=ALU.mult), R=[v_, gCdb], W=[vs_])
        yield
        self.op("dve", lambda e: e.tensor_tensor(qrot_[:], tA_[:, 0:512], tB_[:, 0:512], op=ALU.add), R=[tA_, tB_], W=[qrot_])
        self.op("dve", lambda e: e.tensor_tensor(ktmp_[:], tA_[:, 512:1024], tB_[:, 512:1024], op=ALU.add), R=[tA_, tB_], W=[ktmp_])
        self.op("dve", lambda e: e.tensor_tensor(
            khat_[:].rearrange("p (h e) -> p h e", h=4), ktmp_[:].rearrange("p (h e) -> p h e", h=4),
            cc("ghat").unsqueeze(2).to_broadcast([128, 4, 128]), op=ALU.mult), R=[ktmp_, ca], W=[khat_])
        yield
        pb = self.bankb[4]
        for h in range(4):
            self.op("pe", lambda e, h=h: e.transpose(pb[:, h * 128:(h + 1) * 128], qrot_[:, h * 128:(h + 1) * 128], self.idb[:]),
                    R=[qrot_, self.idb], W=[rb[4]])
        for h in range(4):
            self.op("pe", lambda e, h=h: e.transpose(pb[:, (4 + h) * 128:(5 + h) * 128], khat_[:, h * 128:(h + 1) * 128], self.idb[:]),
                    R=[khat_, self.idb], W=[rb[4]])
        self.op("act", lambda e: e.activation(out=qkT_[:].rearrange("p a t -> p (a t)"), in_=pb[:, 0:1024], func=AF.Copy),
                R=[rb[4]], W=[qkT_])
        yield
        for h in range(4):
            self.op("pe", lambda e, h=h: e.matmul(B[5][:, h * 128:(h + 1) * 128], qkT_[:, 4 + h, :], qkT_[:, h, :],
                                                  start=True, stop=True), R=[qkT_], W=[rb[5]])
        self.op("dve", lambda e: e.tensor_tensor(
            sT_[:].rearrange("p (h c) -> p h c", h=4), B[5][:, :].rearrange("p (h c) -> p h c", h=4),
            maskT[:].unsqueeze(1).to_broadcast([128, 4, 128]), op=ALU.mult), R=[rb[5], maskT], W=[sT_])
        yield

    def stage2(seq, ti, b, xt, par, last):
        tk = slice(b * 128, (b + 1) * 128)
        khat_, qkT_, v_, vs_, sg_, sT_ = khat[par], qkT[par], v_tm[par], vs[par], sg[par], sT[par]
        for h in range(4):
            hs = slice(h * 128, (h + 1) * 128)
            self.op("pe", lambda e, hs=hs: e.matmul(B[6][:, hs], sT_[:, hs], v_[:, hs], start=True, stop=False),
                    R=[sT_, v_], W=[rb[6]])
            self.op("pe", lambda e, hs=hs, h=h: e.matmul(B[6][:, hs], qkT_[:, h, :], Rb[:, hs], start=False, stop=True),
                    R=[qkT_, Rb], W=[rb[6]])
        for h in range(4):
            hs = slice(h * 128, (h + 1) * 128)
            self.op("pe", lambda e, hs=hs: e.matmul(B[7][:, hs], khat_[:, hs], vs_[:, hs], start=True, stop=True),
                    R=[khat_, vs_], W=[rb[7]])
        yield
        self.op("act", lambda e: e.activation(out=ysb[:], in_=B[6][:, :], func=AF.Copy), R=[rb[6]], W=[ysb])
        self.op("dve", lambda e: e.tensor_tensor(R[:], R[:], cc("gC"), op=ALU.mult), R=[R, ca], W=[R])
        self.op("dve", lambda e: e.tensor_tensor(R[:], R[:], B[7][:, :], op=ALU.add), R=[R, rb[7]], W=[R])
        self.op("act", lambda e: e.activation(out=Rb[:], in_=R[:], func=AF.Copy), R=[R], W=[Rb])
        yield
        self.op("dve", lambda e: e.tensor_reduce(st[:, 0:4], ysb[:].rearrange("p (h e) -> p h e", h=4), axis=AX.X, op=ALU.add),
                R=[ysb], W=[st])
        self.op("act", lambda e: e.activation(out=ysq[:], in_=ysb[:], func=AF.Square), R=[ysb], W=[ysq])
        self.op("dve", lambda e: e.tensor_reduce(st[:, 4:8], ysq[:].rearrange("p (h e) -> p h e", h=4), axis=AX.X, op=ALU.add),
                R=[ysq], W=[st])
        yield
        self.op("dve", lambda e: e.tensor_scalar(st[:, 8:12], st[:, 0:4], 1.0 / 128, None, op0=ALU.mult), R=[st], W=[st])
        self.op("dve", lambda e: e.tensor_tensor(st[:, 12:16], st[:, 8:12], st[:, 8:12], op=ALU.mult), R=[st], W=[st])
        self.op("dve", lambda e: e.scalar_tensor_tensor(st[:, 16:20], st[:, 4:8], 1.0 / 128, st[:, 12:16],
                                                        op0=ALU.mult, op1=ALU.subtract), R=[st], W=[st])
        self.op("dve", lambda e: e.tensor_tensor(st[:, 16:20], st[:, 16:20], cc("epsr"), op=ALU.add), R=[st, ca], W=[st])
        self.op("pool", lambda e: e.tensor_tensor(st[:, 20:24], st[:, 16:20], self.mhalf[:, 0:4], op=ALU.pow),
                R=[st, self.mhalf], W=[st])
        yield
        for h in range(4):
            hs = slice(h * 128, (h + 1) * 128)
            self.op("dve", lambda e, h=h, hs=hs: e.tensor_scalar(yc[:, hs], ysb[:, hs], st[:, 8 + h:9 + h], st[:, 20 + h:21 + h],
                                                                 op0=ALU.subtract, op1=ALU.mult), R=[ysb, st], W=[yc])
        self.op("dve", lambda e: e.tensor_tensor(mret[:], yc[:], sg_[:], op=ALU.mult), R=[yc, sg_], W=[mret])
        if seq == 0 and ti == 0 and b == 0:
            self.dump("d_mret", mret)
            self.dump("d_ysb", ysb)
        yield
        pb = self.bankb[6]
        for h in range(4):
            self.op("pe", lambda e, h=h: e.transpose(pb[:, h * 128:(h + 1) * 128], mret[:, h * 128:(h + 1) * 128], self.idb[:]),
                    R=[mret, self.idb], W=[rb[6]])
        self.op("act", lambda e: e.activation(out=mixT[:].rearrange("p k t -> p (k t)"), in_=pb[:, 0:512], func=AF.Copy),
                R=[rb[6]], W=[mixT])
        yield
        for n in range(2):
            bank = 7 - n
            for k in range(4):
                self.op("pe", lambda e, n=n, k=k, bank=bank: e.matmul(B[bank][:, :], mixT[:, k, :], Wo[:, k, n * 512:(n + 1) * 512],
                                                                      start=(k == 0), stop=(k == 3)), R=[mixT, Wo], W=[rb[bank]])
            self.op("dve", lambda e, n=n, bank=bank: e.tensor_tensor(xt[:, b, n * 512:(n + 1) * 512], xt[:, b, n * 512:(n + 1) * 512],
                                                                     B[bank][:, :], op=ALU.add), R=[xt, rb[bank]], W=[xt])
            yield
        if last:
            row0 = seq * self.n_tok + ti * TT
            self.dma(self.x1[row0:row0 + TT, :].rearrange("(b p) d -> p b d", p=128), xt[:], R=[xt])

    def head(seq, ti, xt, hn):
        row0 = seq * self.n_tok + ti * TT
        self.dma(xt[:], self.x[row0:row0 + TT, :].rearrange("(b p) d -> p b d", p=128), W=[xt])
        self.norm_only(xt, NB, xs, hn, ss, vv, rstd, junk, 4)
        yield

    def s1_stream(seq, ti, b, xt, hn, par):
        if b == 0:
            yield from head(seq, ti, xt, hn)
        yield from stage1(seq, ti, b, xt, hn, par)

    def s2_stream(seq, ti, b, xt, par):
        if ti == 0 and b == 0:
            self.op("dve", lambda e: e.memset(R[:], 0.0), W=[R])
            self.op("dve", lambda e: e.memset(Rb[:], 0.0), W=[Rb])
        yield from stage2(seq, ti, b, xt, par, b == NB - 1)

    units = []
    cnt = 0
    for seq in range(self.n_seq):
        for ti in range(self.n_tok // TT):
            for b in range(NB):
                units.append((seq, ti, b, x_tm[cnt % 2], hnT[cnt % 2]))
            cnt += 1
    prev = None
    for i, (seq, ti, b, xt, hn) in enumerate(units):
        par = i % 2
        merge(prev, s1_stream(seq, ti, b, xt, hn, par))
        prev = s2_stream(seq, ti, b, xt, par)
    drain(prev)


Builder.phase1a = _phase1a


def interleave(*gens):
    gens = [g for g in gens if g is not None]
    while gens:
        for g in list(gens):
            try:
                next(g)
            except StopIteration:
                gens.remove(g)
        yield


def _phase1b(self, src1):
    TT, NB, NCH = 256, 2, 4
    Wb = self.T("Wb", [128, 8, 1792], BF16)
    Wo = self.T("WoB", [128, 4, D], BF16)
    W2p = self.T("W2p", [128, RW], BF16)
    A2p = self.T("A2p", [128, RW], BF16)
    G2 = self.T("G2", [128, RW], BF16)
    lnw = self.T("lnw", [128, RW], F32)
    lnb = self.T("lnb", [128, RW], F32)
    cb = self.T("cb", [128, CB1 - CB0], F32)
    self.dma(cb[:], self.cst_d[:, CB0:CB1], W=[cb])
    self.dma(lnw[:], self.pb_d[:, 0:512], W=[lnw])
    self.dma(lnb[:], self.pb_d[:, 512:1024], W=[lnb])

    def cc(name):
        o, w = CS[name]
        return cb[:, o - CB0:o - CB0 + w]

    bonesb = self.T("bonesb", [128, 128], BF16)
    sel2b = self.T("sel2b", [128, 2], BF16)
    self.alloc_stage()
    for k in range(8):
        self.load_w_cols(Wb, lambda lo, hi, k=k: Wb[:, k, lo:hi], self.w_in, k * 128, 2048, 1792,
                         scale_ap=self.pvc("nw_mix", k))
    for k in range(4):
        self.load_w(Wo, Wo[:, k, :], self.w_o[512 + k * 128:512 + (k + 1) * 128, :], D)
    self.op("dve", lambda e: e.memset(W2p[:], 0.0), W=[W2p])
    self.op("dve", lambda e: e.memset(A2p[:], 0.0), W=[A2p])
    self.load_w(W2p, W2p[0:64, :], self.w2, RW, rows=64, prow=0)
    self.load_w(A2p, A2p[64:128, :], self.a2, RW, rows=64, prow=64)
    self.load_w(G2, G2[:, :], self.g2, RW)
    self.op("dve", lambda e: e.tensor_copy(bonesb[:], cc("bones")), R=[cb], W=[bonesb])
    self.op("dve", lambda e: e.tensor_copy(sel2b[:], cc("sel2")), R=[cb], W=[sel2b])
    self.free_stage()

    def two(name, shape, dt):
        return [self.T(name, shape, dt) for _ in range(2)]

    xt = self.T("xb", [128, NB, D], F32)
    x1t = two("x1b", [128, NB, D], F32)
    xs = self.T("xsb", [128, NB, D], BF16)
    hnT = self.T("hnTb", [128, 8, TT], BF16)
    mixT = self.T("mixTb", [128, 4, 128], BF16)
    junk = self.T("junkb", [128, D], BF16)
    ss = self.T("ssb", [128, 4], F32)
    vv = self.T("vvb", [128, 4], F32)
    rstd = self.T("rstdb", [128, 4], F32)
    carry = self.T("carry", [128, 14], F32)
    pm = two("pm", [128, TT], F32)
    h12 = self.T("h12", [128, TT], F32)
    h13 = self.T("h13", [128, TT], F32)
    lo1 = self.T("lo1", [128, TT], BF16)
    lo2 = self.T("lo2", [128, TT], BF16)
    f32names = ["hr", "hk", "lw", "alpha", "cum", "cx", "chh", "ep", "en", "rcp", "kf", "bq"]
    t = {n: self.T(n, [128, TT], F32) for n in f32names}
    bfnames = ["hv", "kk2", "bT", "kT", "bhT", "khT", "rkT"]
    tb = {n: self.T(n, [128, TT], BF16) for n in bfnames}
    ARt = two("ARt", [128, 4, 2, TT], BF16)
    bh_tm = two("bh_tm", [128, NB, RW], BF16)
    kh_tm = two("kh_tm", [128, NB, RW], BF16)
    V_tm = two("V_tm", [128, NB, RW], BF16)
    AT = [[self.T("AT", [128, 4, 512], BF16) for _ in range(4)] for _ in range(2)]
    N1q = two("N1q", [128, 4, 128], BF16)
    Nc = two("Nc", [128, 4, 128], BF16)
    Mc = two("Mc", [128, 4, 128], BF16)
    Qp = two("Qp", [128, 4, 128], BF16)
    Qfin = two("Qfin", [128, 16, 128], BF16)
    gC_all = two("gC_all", [128, NCH, 4], F32)
    g_sb = two("g_sb", [128, NB, RW], F32)
    bs_sb = two("bs_sb", [128, 16], F32)
    Xb = two("Xb", [128, RW], BF16)
    Ub = two("Ub", [128, RW], BF16)
    y_rw = self.T("y_rw", [128, NB, RW], F32)
    S = self.T("Sst", [128, RW], F32)
    tmpS = self.T("tmpS", [128, RW], F32)
    SBD = self.T("SBD", [128, RW], BF16)
    ysq = self.T("ysqb", [128, RW], F32)
    t1 = self.T("t1", [128, RW], F32)
    t2 = self.T("t2", [128, RW], F32)
    mrw = self.T("mrw", [128, RW], BF16)
    hst = self.T("hstb", [128, 56], F32)
    B = self.banks
    rb = self.rb
    BS0 = 384
    for z in Xb + Ub:
        self.op("dve", lambda e, z=z: e.memset(z[:], 0.0), W=[z])

    tsn = [0]

    def tshift(ps, rbank, ci, out_ap, out_tile):
        mu = self.pvc("mu", ci)
        omu = self.dv[:, ci:ci + 1]
        pm_ = pm[tsn[0] % 2]
        tsn[0] += 1
        self.op("act", lambda e: e.activation(out=pm_[:], in_=ps, func=AF.Copy, scale=mu), R=[rbank, self.pv], W=[pm_])
        self.op("dve", lambda e: e.scalar_tensor_tensor(out_ap[:, 1:TT], ps[:, 1:TT], omu, pm_[:, 0:TT - 1],
                                                        op0=ALU.mult, op1=ALU.add), R=[rbank, pm_, self.dv], W=[out_tile])
        self.op("dve", lambda e: e.scalar_tensor_tensor(out_ap[:, 0:1], ps[:, 0:1], omu, carry[:, ci:ci + 1],
                                                        op0=ALU.mult, op1=ALU.add), R=[rbank, carry, self.dv], W=[out_tile])
        self.op("act", lambda e: e.activation(out=carry[:, ci:ci + 1], in_=pm_[:, TT - 1:TT], func=AF.Copy),
                R=[pm_], W=[carry])

    def proj_fm(bank, c0, ci):
        for k in range(8):
            self.op("pe", lambda e, k=k: e.matmul(B[bank][:, c0:c0 + TT], Wb[:, k, ci * 128:(ci + 1) * 128], hnT[:, k, :],
                                                  start=(k == 0), stop=(k == 7)), R=[Wb, hnT], W=[rb[bank]])

    def head(seq, ti, p):
        row0 = seq * self.n_tok + ti * TT
        x1 = x1t[p]
        if ti == 0:
            self.op("dve", lambda e: e.memset(carry[:], 0.0), W=[carry])
        self.dma(xt[:], self.x[row0:row0 + TT, :].rearrange("(b p) d -> p b d", p=128), W=[xt])
        self.dma(x1[:], src1[row0:row0 + TT, :].rearrange("(b p) d -> p b d", p=128), W=[x1])
        self.norm_only(xt, NB, xs, hnT, ss, vv, rstd, junk, 0)
        yield
        proj_fm(2, 0, 12)
        proj_fm(2, TT, 13)
        yield
        tshift(B[2][:, 0:TT], rb[2], 12, h12[:], h12)
        tshift(B[2][:, TT:2 * TT], rb[2], 13, h13[:], h13)
        yield
        self.op("act", lambda e: e.activation(out=lo1[0:64, :], in_=h12[0:64, :], func=AF.Tanh), R=[h12], W=[lo1])
        self.op("act", lambda e: e.activation(out=lo1[64:128, :], in_=h12[64:128, :], func=AF.Copy), R=[h12], W=[lo1])
        self.op("act", lambda e: e.activation(out=h13[:], in_=h13[:], func=AF.Tanh, scale=0.5), R=[h13], W=[h13])
        self.op("dve", lambda e: e.tensor_scalar(lo2[:], h13[:], 0.5, 0.5, op0=ALU.mult, op1=ALU.add), R=[h13], W=[lo2])
        yield
        for blk in range(NB):
            self.op("pe", lambda e, blk=blk: e.matmul(B[3][:, :], lo2[:, blk * 128:(blk + 1) * 128], G2[:], start=True, stop=True),
                    R=[lo2, G2], W=[rb[3]])
            self.op("act", lambda e, blk=blk: e.activation(out=g_sb[p][:, blk, :], in_=B[3][:, :], func=AF.Copy),
                    R=[rb[3]], W=[g_sb[p]])
            yield

    def prepA(j, p, first):
        AT_ = AT[p][j]
        ARt_ = ARt[p]
        js = slice(j * 128, (j + 1) * 128)
        proj_fm(2, 0, j)
        proj_fm(2, TT, 4 + j)
        proj_fm(3, 0, 8 + j)
        yield
        tshift(B[2][:, 0:TT], rb[2], j, t["hr"][:], t["hr"])
        yield
        tshift(B[2][:, TT:2 * TT], rb[2], 4 + j, t["hk"][:], t["hk"])
        yield
        tshift(B[3][:, 0:TT], rb[3], 8 + j, tb["hv"][:], tb["hv"])
        self.op("pe", lambda e: e.matmul(B[0][:, 0:TT], W2p[:, js], lo1[:], start=True, stop=True), R=[W2p, lo1], W=[rb[0]])
        self.op("pe", lambda e: e.matmul(B[0][:, TT:2 * TT], A2p[:, js], lo1[:], start=True, stop=True), R=[A2p, lo1], W=[rb[0]])
        yield
        hw0 = self.dv[:, 14 + j:15 + j]
        ha0 = self.dv[:, 18 + j:19 + j]
        nkk = self.dv[:, 22 + j:23 + j]
        omka = self.dv[:, 26 + j:27 + j]
        k_k = self.pvc("k_k", j)
        k_a = self.pvc("k_a", j)
        r_k = self.pvc("r_k", j)
        lw, alpha, cum, cx, chh, ep, en, rcp, kf, bq = (t[n] for n in ("lw", "alpha", "cum", "cx", "chh", "ep", "en", "rcp", "kf", "bq"))
        self.op("act", lambda e: e.activation(out=lw[:], in_=B[0][:, 0:TT], func=AF.Tanh, bias=hw0, scale=0.5),
                R=[rb[0], self.dv], W=[lw])
        self.op("act", lambda e: e.activation(out=alpha[:], in_=B[0][:, TT:2 * TT], func=AF.Tanh, bias=ha0, scale=0.5),
                R=[rb[0], self.dv], W=[alpha])
        self.op("act", lambda e: e.activation(out=tb["kk2"][:], in_=t["hk"][:], func=AF.Square, scale=k_k),
                R=[t["hk"], self.pv], W=[tb["kk2"]])
        self.op("pe", lambda e: e.matmul(B[3][:, TT:2 * TT], bonesb[:], tb["kk2"][:], start=True, stop=True),
                R=[bonesb, tb["kk2"]], W=[rb[3]])
        yield
        self.op("dve", lambda e: e.tensor_scalar(lw[:], lw[:], 1.0, -0.5 * C_DEC, op0=ALU.add, op1=ALU.mult), R=[lw], W=[lw])
        self.op("dve", lambda e: e.tensor_scalar(alpha[:], alpha[:], 0.5, 0.5, op0=ALU.mult, op1=ALU.add), R=[alpha], W=[alpha])
        self.op("dve", lambda e: e.tensor_tensor_scan(cum[:], cc("scanmask"), lw[:], 0.0, op0=ALU.mult, op1=ALU.add),
                R=[cb, lw], W=[cum])
        yield
        self.op("dve", lambda e: e.tensor_tensor(cx[:], cum[:], lw[:], op=ALU.subtract), R=[cum, lw], W=[cx])
        cumC = cum[:].rearrange("p (c s) -> p c s", s=64)[:, :, 63:64].to_broadcast([128, NCH, 64])
        self.op("dve", lambda e: e.tensor_tensor(chh[:].rearrange("p (c s) -> p c s", s=64), cumC,
                                                 cum[:].rearrange("p (c s) -> p c s", s=64), op=ALU.subtract), R=[cum], W=[chh])
        self.op("act", lambda e: e.activation(out=ep[:], in_=cum[:], func=AF.Exp), R=[cum], W=[ep])
        self.op("act", lambda e: e.activation(out=en[:], in_=cum[:], func=AF.Exp, scale=-1.0), R=[cum], W=[en])
        self.op("act", lambda e: e.activation(out=cx[:], in_=cx[:], func=AF.Exp), R=[cx], W=[cx])
        self.op("act", lambda e: e.activation(out=chh[:], in_=chh[:], func=AF.Exp), R=[chh], W=[chh])
        self.op("act", lambda e: e.activation(out=gC_all[p][:, :, j], in_=ep[:].rearrange("p (c s) -> p c s", s=64)[:, :, 63],
                                               func=AF.Copy), R=[ep], W=[gC_all[p]])
        yield
        self.op("dve", lambda e: e.tensor_scalar(rcp[:], B[3][:, TT:2 * TT], 1e-24, None, op0=ALU.max), R=[rb[3]], W=[rcp])
        self.op("dve", lambda e: e.reciprocal(rcp[:], rcp[:]), R=[rcp], W=[rcp])
        yield
        self.op("dve", lambda e: e.tensor_scalar(kf[:], alpha[:], k_a, omka, op0=ALU.mult, op1=ALU.add),
                R=[alpha, self.pv, self.dv], W=[kf])
        self.op("pool", lambda e: e.tensor_tensor(kf[:], t["hk"][:], kf[:], op=ALU.mult), R=[t["hk"], kf], W=[kf])
        self.op("dve", lambda e: e.scalar_tensor_tensor(bq[:], t["hk"][:], k_k, alpha[:], op0=ALU.mult, op1=ALU.mult),
                R=[t["hk"], alpha, self.pv], W=[bq])
        self.op("pool", lambda e: e.tensor_tensor(bq[:], bq[:], rcp[:], op=ALU.mult), R=[bq, rcp], W=[bq])
        yield
        self.op("dve", lambda e: e.scalar_tensor_tensor(ARt_[:, j, 0, :], t["hk"][:], nkk, cx[:], op0=ALU.mult, op1=ALU.mult),
                R=[t["hk"], cx, self.dv], W=[ARt_])
        self.op("pool", lambda e: e.tensor_tensor(ARt_[:, j, 1, :], t["hr"][:], ep[:], op=ALU.mult), R=[t["hr"], ep], W=[ARt_])
        self.op("dve", lambda e: e.tensor_tensor(tb["bT"][:], bq[:], en[:], op=ALU.mult), R=[bq, en], W=[tb["bT"]])
        self.op("dve", lambda e: e.tensor_tensor(tb["kT"][:], kf[:], en[:], op=ALU.mult), R=[kf, en], W=[tb["kT"]])
        yield
        self.op("pool", lambda e: e.tensor_tensor(tb["bhT"][:], bq[:], chh[:], op=ALU.mult), R=[bq, chh], W=[tb["bhT"]])
        self.op("pool", lambda e: e.tensor_tensor(tb["khT"][:], kf[:], chh[:], op=ALU.mult), R=[kf, chh], W=[tb["khT"]])
        self.op("dve", lambda e: e.scalar_tensor_tensor(tb["rkT"][:], t["hr"][:], r_k, kf[:], op0=ALU.mult, op1=ALU.mult),
                R=[t["hr"], kf, self.pv], W=[tb["rkT"]])
        if first and j == 0:
            for nm in ("lw", "alpha", "cum", "kf", "bq", "hr", "hk"):
                self.dump("d_" + nm, t[nm])
            self.dump("d_hv", tb["hv"])
        yield
        for hb in range(2):
            hr_ = slice(hb * 64, (hb + 1) * 64)
            for blk in range(NB):
                u = hb * 2 + blk
                bank = u % 2
                tk = slice(blk * 128, (blk + 1) * 128)
                self.op("pe", lambda e, bank=bank, hr_=hr_, tk=tk: e.matmul(B[bank][:, 0:256], tb["bT"][hr_, tk], ARt_[hr_, j, :, tk],
                                                                            start=True, stop=True), R=[tb["bT"], ARt_], W=[rb[bank]])
                self.op("pe", lambda e, bank=bank, hr_=hr_, tk=tk: e.matmul(B[bank][:, 256:512], tb["kT"][hr_, tk], ARt_[hr_, j, :, tk],
                                                                            start=True, stop=True), R=[tb["kT"], ARt_], W=[rb[bank]])
                self.op("dve", lambda e, bank=bank, u=u: e.tensor_tensor(AT_[:, u, :], B[bank][:, :], cc("mask4"), op=ALU.mult),
                        R=[rb[bank], cb], W=[AT_])
                yield
        pb = self.bankb[1]
        for qi, srcT in enumerate((tb["bhT"], tb["khT"], tb["hv"])):
            for blk in range(NB):
                i = qi * 2 + blk
                self.op("pe", lambda e, i=i, blk=blk, srcT=srcT: e.transpose(pb[:, i * 128:(i + 1) * 128],
                                                                             srcT[:, blk * 128:(blk + 1) * 128], self.idb[:]),
                        R=[srcT, self.idb], W=[rb[1]])
        for blk in range(NB):
            c0 = BS0 + blk * 2
            self.op("pe", lambda e, blk=blk, c0=c0: e.matmul(B[1][:, c0:c0 + 2], tb["rkT"][:, blk * 128:(blk + 1) * 128], sel2b[:],
                                                             start=True, stop=True), R=[tb["rkT"], sel2b], W=[rb[1]])
        for qi, dst in enumerate((bh_tm[p], kh_tm[p], V_tm[p])):
            self.op("act", lambda e, qi=qi, dst=dst: e.activation(
                out=dst[:, :, js], in_=pb[:, qi * 256:(qi + 1) * 256].rearrange("p (b c) -> p b c", b=2), func=AF.Copy),
                R=[rb[1]], W=[dst])
        self.op("dve", lambda e: e.tensor_copy(
            bs_sb[p][:].rearrange("p (b h) -> p b h", b=2)[:, :, 2 * j:2 * j + 2],
            B[1][:, BS0:BS0 + 4].rearrange("p (b h) -> p b h", b=2)), R=[rb[1]], W=[bs_sb[p]])
        yield
        for u in range(4):
            self.op("pe", lambda e, u=u: e.transpose(pb[:, u * 128:(u + 1) * 128], AT_[:, u, 0:128], self.idb[:]),
                    R=[AT_, self.idb], W=[rb[1]])
        self.op("act", lambda e: e.activation(out=N1q[j % 2][:].rearrange("p u t -> p (u t)"), in_=pb[:, 0:512], func=AF.Copy),
                R=[rb[1]], W=[N1q[j % 2]])
        yield

    def neumann(j, p):
        AT_ = AT[p][j]
        N1_ = N1q[j % 2]
        self.op("dve", lambda e: e.tensor_tensor(Qp[0][:], AT_[:, :, 0:128],
                                                 self.idb[:].unsqueeze(1).to_broadcast([128, 4, 128]), op=ALU.add),
                R=[AT_, self.idb], W=[Qp[0]])
        yield
        Mprev = lambda u: AT_[:, u, 0:128]
        Nprev = lambda u: N1_[:, u, :]
        Mpt, Npt = AT_, N1_
        qi = 0
        for lev in range(1, 6):
            Ncur, Mcur = Nc[lev % 2], Mc[lev % 2]
            for u in range(4):
                self.op("pe", lambda e, u=u, Mprev=Mprev, Nprev=Nprev: e.matmul(B[4][:, u * 128:(u + 1) * 128], Mprev(u), Nprev(u),
                                                                              start=True, stop=True), R=[Mpt, Npt], W=[rb[4]])
            if lev < 5:
                for u in range(4):
                    self.op("pe", lambda e, u=u, Mprev=Mprev, Nprev=Nprev: e.matmul(B[5][:, u * 128:(u + 1) * 128], Nprev(u), Mprev(u),
                                                                                  start=True, stop=True), R=[Mpt, Npt], W=[rb[5]])
            yield
            self.op("act", lambda e, Ncur=Ncur: e.activation(out=Ncur[:].rearrange("p u t -> p (u t)"), in_=B[4][:, :], func=AF.Copy),
                    R=[rb[4]], W=[Ncur])
            if lev < 5:
                self.op("act", lambda e, Mcur=Mcur: e.activation(out=Mcur[:].rearrange("p u t -> p (u t)"), in_=B[5][:, :], func=AF.Copy),
                        R=[rb[5]], W=[Mcur])
            yield
            Qprev = Qp[qi]
            for u in range(4):
                self.op("pe", lambda e, u=u, Ncur=Ncur, Qprev=Qprev: e.matmul(B[4][:, u * 128:(u + 1) * 128], Ncur[:, u, :], Qprev[:, u, :],
                                                                            start=True, stop=True), R=[Ncur, Qprev], W=[rb[4]])
            yield
            if lev < 5:
                Qn = Qp[1 - qi]
                self.op("dve", lambda e, Qn=Qn, Qprev=Qprev: e.tensor_tensor(Qn[:].rearrange("p u t -> p (u t)"),
                                                                             Qprev[:].rearrange("p u t -> p (u t)"), B[4][:, :], op=ALU.add),
                        R=[Qprev, rb[4]], W=[Qn])
                qi = 1 - qi
            else:
                self.op("dve", lambda e, Qprev=Qprev: e.tensor_tensor(Qfin[p][:, j * 4:(j + 1) * 4, :],
                                                                      Qprev[:], B[4][:, :].rearrange("p (u t) -> p u t", u=4), op=ALU.add),
                        R=[Qprev, rb[4]], W=[Qfin[p]])
            yield
            Mprev = lambda u, Mcur=Mcur: Mcur[:, u, :]
            Nprev = lambda u, Ncur=Ncur: Ncur[:, u, :]
            Mpt, Npt = Mcur, Ncur

    def chain(c, p):
        blk, half = c // 2, c % 2
        hs = slice(half * 64, half * 64 + 64)
        tcs = slice(c * 64, (c + 1) * 64)
        ARt_, Vt, Qf, Xh, Uh = ARt[p], V_tm[p], Qfin[p], Xb[half], Ub[half]
        for j in range(4):
            js = slice(j * 128, (j + 1) * 128)
            self.op("pe", lambda e, j=j, js=js: e.matmul(B[6][hs, js], ARt_[:, j, 0, tcs], SBD[:, js], start=True, stop=False),
                    R=[ARt_, SBD], W=[rb[6]])
            for hb in range(2):
                u = hb * 2 + blk
                cs = slice(j * 128 + hb * 64, j * 128 + hb * 64 + 64)
                self.op("pe", lambda e, j=j, u=u, cs=cs, hb=hb: e.matmul(
                    B[6][hs, cs], AT[p][j][:, u, 256 + half * 64:256 + half * 64 + 64], Vt[:, blk, cs],
                    start=False, stop=(hb == 1)), R=[AT[p][j], Vt], W=[rb[6]])
        yield
        self.op("act", lambda e: e.activation(out=Xh[hs, :], in_=B[6][hs, :], func=AF.Copy), R=[rb[6]], W=[Xh])
        yield
        for j in range(4):
            for hb in range(2):
                u = hb * 2 + blk
                cs = slice(j * 128 + hb * 64, j * 128 + hb * 64 + 64)
                self.op("pe", lambda e, j=j, u=u, cs=cs: e.matmul(B[6][hs, cs], Qf[:, j * 4 + u, half * 64:half * 64 + 64], Xh[:, cs],
                                                                  start=True, stop=True), R=[Qf, Xh], W=[rb[6]])
        yield
        self.op("dve", lambda e: e.tensor_copy(Uh[hs, :], B[6][hs, :]), R=[rb[6]], W=[Uh])
        yield
        for j in range(4):
            js = slice(j * 128, (j + 1) * 128)
            self.op("pe", lambda e, js=js: e.matmul(B[6][:, js], bh_tm[p][hs, blk, js], Uh[hs, js], start=True, stop=False),
                    R=[bh_tm[p], Uh], W=[rb[6]])
            self.op("pe", lambda e, js=js: e.matmul(B[6][:, js], kh_tm[p][hs, blk, js], Vt[hs, blk, js], start=False, stop=True),
                    R=[kh_tm[p], Vt], W=[rb[6]])
        for j in range(4):
            js = slice(j * 128, (j + 1) * 128)
            self.op("pe", lambda e, j=j, js=js: e.matmul(B[7][hs, js], ARt_[:, j, 1, tcs], SBD[:, js], start=True, stop=False),
                    R=[ARt_, SBD], W=[rb[7]])
            for hb in range(2):
                u = hb * 2 + blk
                cs = slice(j * 128 + hb * 64, j * 128 + hb * 64 + 64)
                self.op("pe", lambda e, j=j, u=u, cs=cs: e.matmul(
                    B[7][hs, cs], AT[p][j][:, u, 128 + half * 64:128 + half * 64 + 64], Uh[:, cs], start=False, stop=False),
                    R=[AT[p][j], Uh], W=[rb[7]])
                self.op("pe", lambda e, j=j, u=u, cs=cs, hb=hb: e.matmul(
                    B[7][hs, cs], AT[p][j][:, u, 384 + half * 64:384 + half * 64 + 64], Vt[:, blk, cs],
                    start=False, stop=(hb == 1)), R=[AT[p][j], Vt], W=[rb[7]])
        yield
        self.op("dve", lambda e: e.tensor_tensor(tmpS[:], B[6][:, :], cc("bd4"), op=ALU.mult), R=[rb[6], cb], W=[tmpS])
        self.op("dve", lambda e: e.tensor_tensor(S[:].rearrange("p (j v) -> p j v", j=4), S[:].rearrange("p (j v) -> p j v", j=4),
                                                 gC_all[p][:, c, :].unsqueeze(2).to_broadcast([128, 4, 128]), op=ALU.mult),
                R=[S, gC_all[p]], W=[S])
        yield
        self.op("dve", lambda e: e.tensor_tensor(S[:], S[:], tmpS[:], op=ALU.add), R=[S, tmpS], W=[S])
        self.op("act", lambda e: e.activation(out=y_rw[hs, blk, :], in_=B[7][hs, :], func=AF.Copy), R=[rb[7]], W=[y_rw])
        yield
        self.op("act", lambda e: e.activation(out=SBD[:], in_=S[:], func=AF.Copy), R=[S], W=[SBD])
        yield

    def finish(blk, p):
        x1 = x1t[p]
        Vt = V_tm[p]
        y = y_rw[:, blk, :]
        y3 = y.rearrange("p (h n) -> p h n", h=8)
        self.op("dve", lambda e: e.tensor_reduce(hst[:, 0:8], y3, axis=AX.X, op=ALU.add), R=[y_rw], W=[hst])
        self.op("act", lambda e: e.activation(out=ysq[:], in_=y, func=AF.Square), R=[y_rw], W=[ysq])
        yield
        self.op("dve", lambda e: e.tensor_reduce(hst[:, 8:16], ysq[:].rearrange("p (h n) -> p h n", h=8), axis=AX.X, op=ALU.add),
                R=[ysq], W=[hst])
        self.op("dve", lambda e: e.tensor_scalar(hst[:, 16:24], hst[:, 0:8], 1.0 / 64, None, op0=ALU.mult), R=[hst], W=[hst])
        self.op("dve", lambda e: e.tensor_tensor(hst[:, 24:32], hst[:, 16:24], hst[:, 16:24], op=ALU.mult), R=[hst], W=[hst])
        self.op("dve", lambda e: e.scalar_tensor_tensor(hst[:, 32:40], hst[:, 8:16], 1.0 / 64, hst[:, 24:32],
                                                        op0=ALU.mult, op1=ALU.subtract), R=[hst], W=[hst])
        self.op("dve", lambda e: e.tensor_scalar(hst[:, 32:40], hst[:, 32:40], 64e-5, None, op0=ALU.add), R=[hst], W=[hst])
        yield
        self.op("pool", lambda e: e.tensor_tensor(hst[:, 40:48], hst[:, 32:40], self.mhalf[:, 0:8], op=ALU.pow),
                R=[hst, self.mhalf], W=[hst])
        self.op("pool", lambda e: e.tensor_tensor(t2[:].rearrange("p (h n) -> p h n", h=8),
                                                  Vt[:, blk, :].rearrange("p (h n) -> p h n", h=8),
                                                  bs_sb[p][:, blk * 8:(blk + 1) * 8].unsqueeze(2).to_broadcast([128, 8, 64]), op=ALU.mult),
                R=[Vt, bs_sb[p]], W=[t2])
        yield
        self.op("dve", lambda e: e.scalar_tensor_tensor(hst[:, 48:56], hst[:, 16:24], -1.0, hst[:, 40:48], op0=ALU.mult, op1=ALU.mult),
                R=[hst], W=[hst])
        yield
        for h in range(8):
            hsl = slice(h * 64, (h + 1) * 64)
            self.op("act", lambda e, h=h, hsl=hsl: e.activation(out=t1[:, hsl], in_=y_rw[:, blk, hsl], func=AF.Identity,
                                                                bias=hst[:, 48 + h:49 + h], scale=hst[:, 40 + h:41 + h]),
                    R=[y_rw, hst], W=[t1])
        yield
        self.op("dve", lambda e: e.tensor_tensor(t1[:], t1[:], lnw[:], op=ALU.mult), R=[t1, lnw], W=[t1])
        self.op("dve", lambda e: e.tensor_tensor(t1[:], t1[:], lnb[:], op=ALU.add), R=[t1, lnb], W=[t1])
        yield
        self.op("dve", lambda e: e.tensor_tensor(t1[:], t1[:], t2[:], op=ALU.add), R=[t1, t2], W=[t1])
        self.op("dve", lambda e: e.tensor_tensor(mrw[:], t1[:], g_sb[p][:, blk, :], op=ALU.mult), R=[t1, g_sb[p]], W=[mrw])
        yield
        pb = self.bankb[6]
        for k in range(4):
            self.op("pe", lambda e, k=k: e.transpose(pb[:, k * 128:(k + 1) * 128], mrw[:, k * 128:(k + 1) * 128], self.idb[:]),
                    R=[mrw, self.idb], W=[rb[6]])
        self.op("act", lambda e: e.activation(out=mixT[:].rearrange("p k t -> p (k t)"), in_=pb[:, 0:512], func=AF.Copy),
                R=[rb[6]], W=[mixT])
        yield
        for n in range(2):
            bank = 7 - n
            for k in range(4):
                self.op("pe", lambda e, n=n, k=k, bank=bank: e.matmul(B[bank][:, :], mixT[:, k, :], Wo[:, k, n * 512:(n + 1) * 512],
                                                                      start=(k == 0), stop=(k == 3)), R=[mixT, Wo], W=[rb[bank]])
            self.op("dve", lambda e, n=n, bank=bank: e.tensor_tensor(x1[:, blk, n * 512:(n + 1) * 512], x1[:, blk, n * 512:(n + 1) * 512],
                                                                     B[bank][:, :], op=ALU.add), R=[x1, rb[bank]], W=[x1])
            yield

    def chain_gens(*gens):
        for g in gens:
            if g is not None:
                yield from g

    tiles = [(seq, ti) for seq in range(self.n_seq) for ti in range(self.n_tok // TT)]

    def streamP_all():
        pending = None
        for i, (seq, ti) in enumerate(tiles):
            p = i % 2
            first = (i == 0)
            yield ("start", i)
            for _ in interleave(pending, chain_gens(head(seq, ti, p), prepA(0, p, first))):
                yield ("step", i)
            if i > 0:
                yield ("ready", i - 1)
            for j in range(3):
                for _ in interleave(neumann(j, p), prepA(j + 1, p, first)):
                    yield ("step", i)
            pending = neumann(3, p)
        for _ in pending:
            yield ("step", len(tiles) - 1)
        yield ("ready", len(tiles) - 1)

    def streamC(i):
        seq, ti = tiles[i]
        p = i % 2
        row0 = seq * self.n_tok + ti * TT
        if ti == 0:
            self.op("dve", lambda e: e.memset(S[:], 0.0), W=[S])
            self.op("dve", lambda e: e.memset(SBD[:], 0.0), W=[SBD])
        for c in range(NCH):
            yield from chain(c, p)
        if i == 0:
            self.dump("d_yrw", y_rw)
        for blk in range(NB):
            yield from finish(blk, p)
        self.dma(self.xm[row0:row0 + TT, :].rearrange("(b p) d -> p b d", p=128), x1t[p][:], R=[x1t[p]])

    Cact = None
    Cidx = -1
    for kind, i in streamP_all():
        if kind == "start":
            if Cact is not None and Cidx <= i - 2:
                drain(Cact)
                Cact = None
        elif kind == "ready":
            drain(Cact)
            Cact = streamC(i)
            Cidx = i
        if Cact is not None:
            try:
                next(Cact)
            except StopIteration:
                Cact = None
    drain(Cact)


Builder.phase1b = _phase1b
```
